# Optimizing a Trainium2 kernel written in Bass

```python
import math
import jax
import jax.numpy as jnp
from jax import lax
import numpy as np

D_MODEL = 1024
BATCH = 4
SEQ = 8192
DEPTH = 1

GRID_W = 64
CTX_LEN = 256
NH_A = 8
DK_A = 64
DV_A = 128
D_A = NH_A * DV_A
CHUNK = 128
GATE_CAP = 15.0
NH_B = 16
DH_B = 64
D_B = NH_B * DH_B
WIN_R = 8
WIN_C = 16
QB_R_MAX = 8
QB_C = 16
D_FF = -(-8 * D_MODEL // (3 * 256)) * 256
ROPE_THETA = 10000.0
EPS = 1e-6
N_MOD = 6
_PART_NAMES = ('q_a', 'k_a', 'v_a', 'o_a', 'gates_a', 'q_b', 'k_b', 'v_b', 'merge')
_PART_SIZES = (NH_A * DK_A, NH_A * DK_A, D_A, D_A, 4 * NH_A, D_B, D_B, D_B, 2 * D_MODEL)
D_IN = sum(_PART_SIZES)

kernel_name = 'hybrid_mlstm_natten_dit_layer'


def _rmsnorm(x, g):
    xf = x.astype(jnp.float32)
    y = xf * lax.rsqrt(jnp.mean(xf * xf, axis=-1, keepdims=True) + EPS)
    return (y * g.astype(jnp.float32)).astype(x.dtype)


def _proj(h, w_in, names):
    offs = np.cumsum((0,) + _PART_SIZES)
    sel = [i for i, n in enumerate(_PART_NAMES) if n in names]
    if len(sel) == len(_PART_NAMES):
        w = w_in
    else:
        w = jnp.concatenate([w_in[:, int(offs[i]):int(offs[i + 1])] for i in sel], axis=1)
    sizes = [_PART_SIZES[i] for i in sel]
    parts = jnp.split(h @ w, [int(s) for s in np.cumsum(sizes)[:-1]], axis=-1)
    return {_PART_NAMES[i]: p for i, p in zip(sel, parts)}


def _rope_axis(x, pos):
    half = x.shape[-1] // 2
    inv = ROPE_THETA ** (-jnp.arange(half, dtype=jnp.float32) / half)
    ang = pos.astype(jnp.float32)[:, None] * inv
    cos, sin = jnp.cos(ang)[:, None, :], jnp.sin(ang)[:, None, :]
    x1 = x[..., :half].astype(jnp.float32)
    x2 = x[..., half:].astype(jnp.float32)
    return jnp.concatenate([x1 * cos - x2 * sin, x2 * cos + x1 * sin], axis=-1).astype(x.dtype)


def _rope2d(x, row, col):
    d = x.shape[-1] // 2
    return jnp.concatenate([_rope_axis(x[..., :d], row), _rope_axis(x[..., d:], col)], axis=-1)


def _zero_state(b):
    return (jnp.zeros((b, NH_A, DK_A, DV_A), jnp.float32),
            jnp.zeros((b, NH_A, DK_A), jnp.float32),
            jnp.zeros((b, NH_A), jnp.float32))


def _mlstm_inputs(p, b_g, pos):
    bn, L, _ = p['q_a'].shape
    q = p['q_a'].reshape(bn, L, NH_A, DK_A)
    k = p['k_a'].reshape(bn, L, NH_A, DK_A)
    if pos is not None:
        q = _rope2d(q, pos[0], pos[1])
        k = _rope2d(k, pos[0], pos[1])
    v = p['v_a'].reshape(bn, L, NH_A, DV_A)
    bhl = lambda t: t.astype(jnp.float32).transpose(0, 2, 1, 3)
    g = p['gates_a'].reshape(bn, L, 4, NH_A).astype(jnp.float32) + b_g.astype(jnp.float32)
    g = GATE_CAP * jnp.tanh(g / GATE_CAP)
    g = g.transpose(2, 0, 3, 1)
    return (bhl(q) * DK_A ** -0.5, bhl(k), bhl(v),
            g[0], jax.nn.log_sigmoid(g[1]), g[2], jax.nn.log_sigmoid(g[3]))


def _mlstm_scan(q, k, v, ig, lf, state, with_output):
    bn, nh, L, _ = q.shape
    lc = min(CHUNK, L)
    nc = L // lc

    def to_chunks(t):
        t = t.reshape(t.shape[:2] + (nc, lc) + t.shape[3:])
        return jnp.moveaxis(t, 2, 0)

    causal = jnp.tril(jnp.ones((lc, lc), dtype=bool))

    def body(carry, xs):
        C, n, m = carry
        qc, kc, vc, igc, lfc = xs
        b = jnp.cumsum(lfc, axis=-1)
        b_last = b[..., -1]
        g_end = b_last[..., None] - b + igc
        m_new = jnp.maximum(b_last + m, jnp.max(g_end, axis=-1))
        w_end = jnp.exp(g_end - m_new[..., None])
        decay = jnp.exp(b_last + m - m_new)
        kw = kc * w_end[..., None]
        C_new = decay[..., None, None] * C + jnp.einsum('bhsd,bhse->bhde', kw, vc)
        n_new = decay[..., None] * n + jnp.sum(kw, axis=2)
        if not with_output:
            return (C_new, n_new, m_new), None
        log_d = jnp.where(causal, b[..., :, None] - b[..., None, :] + igc[..., None, :], -jnp.inf)
        log_inter = b + m[..., None]
        m_t = jnp.maximum(log_inter, jnp.max(log_d, axis=-1))
        a = jnp.exp(log_inter - m_t)
        s = jnp.einsum('bhtd,bhsd->bhts', qc, kc) * jnp.exp(log_d - m_t[..., None])
        num = a[..., None] * jnp.einsum('bhtd,bhde->bhte', qc, C) + jnp.einsum('bhts,bhse->bhte', s, vc)
        den = a * jnp.einsum('bhtd,bhd->bht', qc, n) + jnp.sum(s, axis=-1)
        h = num / jnp.maximum(jnp.abs(den), jnp.exp(-m_t))[..., None]
        return (C_new, n_new, m_new), h

    state, h = lax.scan(body, state, tuple(to_chunks(t) for t in (q, k, v, ig, lf)))
    if with_output:
        h = jnp.moveaxis(h, 0, 2).reshape(bn, nh, L, DV_A)
    return h, state


def _mlstm_bidir(inp, states0, with_output):
    q, k, v, ig_f, lf_f, ig_b, lf_b = inp
    h_f, st_f = _mlstm_scan(q, k, v, ig_f, lf_f, states0[0], with_output)
    fl = lambda t: jnp.flip(t, axis=2)
    h_b, st_b = _mlstm_scan(fl(q), fl(k), fl(v), fl(ig_b), fl(lf_b), states0[1], with_output)
    h = h_f + fl(h_b) if with_output else None
    return h, st_f, st_b


def _qk_norm(t, g):
    bn, L, _ = t.shape
    return _rmsnorm(t.reshape(bn, L, NH_B, DH_B), g)


def _axis_blocks(n, k, qb):
    kb = min(k + qb - 1, n)
    q0 = np.arange(0, n, qb)
    kstart = np.clip(q0 - k // 2, 0, n - kb)
    qpos = q0[:, None] + np.arange(qb)
    kpos = kstart[:, None] + np.arange(kb)
    wstart = np.clip(qpos - k // 2, 0, n - k)
    inwin = (kpos[:, None, :] >= wstart[:, :, None]) & (kpos[:, None, :] < wstart[:, :, None] + k)
    off = kpos[:, None, :] - qpos[:, :, None]
    return kpos, inwin, off


def _neighbourhood_attention(q, k, v, kc, vc, rpb, rows):
    bn, S, nh, d = q.shape
    kh = min(WIN_R, rows)
    qr = math.gcd(rows, QB_R_MAX)
    r_k, r_in, r_off = _axis_blocks(rows, kh, qr)
    c_k, c_in, c_off = _axis_blocks(GRID_W, WIN_C, QB_C)
    nbr, nbc = rows // qr, GRID_W // QB_C
    kr, kcol = r_k.shape[1], c_k.shape[1]
    nq, nk = qr * QB_C, kr * kcol
    ktok = (r_k[:, None, :, None] * GRID_W + c_k[None, :, None, :]).reshape(nbr, nbc, nk)
    mask = (r_in[:, None, :, None, :, None] & c_in[None, :, None, :, None, :]).reshape(nbr, nbc, nq, nk)
    dr = np.clip(r_off + WIN_R - 1, 0, 2 * WIN_R - 2)
    dc = np.clip(c_off + WIN_C - 1, 0, 2 * WIN_C - 2)
    qb = q.reshape(bn, nbr, qr, nbc, QB_C, nh, d).transpose(1, 0, 3, 2, 4, 5, 6).reshape(nbr, bn, nbc, nq, nh, d)

    def row_block(args):
        q_x, ktok_x, mask_x, dr_x = args
        kb = k[:, ktok_x]
        vb = v[:, ktok_x]
        bias = rpb[:, dr_x[None, :, None, :, None], dc[:, None, :, None, :]]
        bias = jnp.where(mask_x, bias.reshape(nh, nbc, nq, nk).astype(jnp.float32), -jnp.inf)
        s_win = jnp.einsum('byqhd,bykhd->bhyqk', q_x, kb).astype(jnp.float32) + bias[None]
        s_ctx = jnp.einsum('byqhd,bchd->bhyqc', q_x, kc).astype(jnp.float32)
        p = jax.nn.softmax(jnp.concatenate([s_win, s_ctx], axis=-1), axis=-1)
        return (jnp.einsum('bhyqk,bykhd->byqhd', p[..., :nk].astype(v.dtype), vb)
                + jnp.einsum('bhyqc,bchd->byqhd', p[..., nk:].astype(vc.dtype), vc))

    o = lax.map(row_block, (qb, jnp.asarray(ktok, jnp.int32), jnp.asarray(mask), jnp.asarray(dr, jnp.int32)))
    o = o.reshape(nbr, bn, nbc, qr, QB_C, nh, d).transpose(1, 0, 3, 2, 4, 5, 6)
    return o.reshape(bn, S, nh * d)


def _ctx_attention(q, k, v):
    bn, L, nh, d = q.shape
    s = jnp.einsum('bqhd,bkhd->bhqk', q, k).astype(jnp.float32)
    p = jax.nn.softmax(s, axis=-1).astype(v.dtype)
    return jnp.einsum('bhqk,bkhd->bqhd', p, v).reshape(bn, L, nh * d)


def _merge(h_a, o_gate, na_out, merge, g_norm, w_a, w_b, w_o):
    bn, nh, L, dv = h_a.shape
    ha = _rmsnorm(h_a.transpose(0, 2, 1, 3), g_norm.reshape(NH_A, DV_A)).reshape(bn, L, D_A)
    ha = ha.astype(o_gate.dtype) * jax.nn.sigmoid(o_gate)
    g_a, g_b = jnp.split(jax.nn.sigmoid(merge), 2, axis=-1)
    return (g_a * (ha @ w_a) + g_b * (na_out @ w_b)) @ w_o


def _swiglu(h, wg, wu, wd):
    return (jax.nn.silu(h @ wg) * (h @ wu)) @ wd


def setup_inputs(seed: int = 0) -> dict:
    key = jax.random.key(seed)
    ks = jax.random.split(key, 20)
    D = D_MODEL
    nrm = lambda kk, shape, scale: jax.random.normal(kk, shape, jnp.float32) * scale
    forget_bias = jnp.linspace(3.0, 6.0, NH_A, dtype=jnp.float32)
    gate_rows = jnp.array([0.0, 1.0, 0.0, 1.0], jnp.float32)[:, None]
    return {
        'x': nrm(ks[0], (BATCH, SEQ, D), 1.0),
        'c': nrm(ks[1], (BATCH, D), 1.0),
        'ctx': nrm(ks[2], (BATCH, CTX_LEN, D), 1.0),
        'c_ctx': nrm(ks[3], (D,), 1.0),
        'w_mod': nrm(ks[4], (DEPTH, D, N_MOD * D), 0.5 * D ** -0.5),
        'b_mod': nrm(ks[5], (DEPTH, N_MOD * D), 0.02),
        'norm1_g': 1.0 + nrm(ks[6], (DEPTH, D), 0.02),
        'w_in': nrm(ks[7], (DEPTH, D, D_IN), D ** -0.5),
        'b_gates': nrm(ks[8], (DEPTH, 4, NH_A), 0.1) + gate_rows * forget_bias,
        'mlstm_norm_g': 1.0 + nrm(ks[9], (DEPTH, D_A), 0.02),
        'qn_g': 1.0 + nrm(ks[10], (DEPTH, DH_B), 0.02),
        'kn_g': 1.0 + nrm(ks[11], (DEPTH, DH_B), 0.02),
        'rpb': nrm(ks[12], (DEPTH, NH_B, 2 * WIN_R - 1, 2 * WIN_C - 1), 0.1),
        'w_branch_a': nrm(ks[13], (DEPTH, D_A, D), D_A ** -0.5),
        'w_branch_b': nrm(ks[14], (DEPTH, D_B, D), D_B ** -0.5),
        'w_out': nrm(ks[15], (DEPTH, D, D), D ** -0.5),
        'norm2_g': 1.0 + nrm(ks[16], (DEPTH, D), 0.02),
        'w_ffn_gate': nrm(ks[17], (DEPTH, D, D_FF), D ** -0.5),
        'w_ffn_up': nrm(ks[18], (DEPTH, D, D_FF), D ** -0.5),
        'w_ffn_down': nrm(ks[19], (DEPTH, D_FF, D), D_FF ** -0.5),
    }


def reference(x, c, ctx, c_ctx, w_mod, b_mod, norm1_g, w_in, b_gates, mlstm_norm_g, qn_g, kn_g,
              rpb, w_branch_a, w_branch_b, w_out, norm2_g, w_ffn_gate, w_ffn_up, w_ffn_down):
    bn, S, _ = x.shape
    L_ctx = ctx.shape[1]
    rows = S // GRID_W
    t = jnp.arange(S)
    pos = (t // GRID_W, t % GRID_W)
    for l in range(DEPTH):
        last = l == DEPTH - 1
        sh1, sc1, g1, sh2, sc2, g2 = jnp.split(jax.nn.silu(c) @ w_mod[l] + b_mod[l], N_MOD, axis=-1)
        sh1c, sc1c, g1c, sh2c, sc2c, g2c = jnp.split(jax.nn.silu(c_ctx) @ w_mod[l] + b_mod[l], N_MOD, axis=-1)
        hx = _rmsnorm(x, norm1_g[l]) * (1.0 + sc1[:, None]) + sh1[:, None]
        hc = _rmsnorm(ctx, norm1_g[l]) * (1.0 + sc1c) + sh1c
        px = _proj(hx, w_in[l], _PART_NAMES)
        ctx_parts = _PART_NAMES if not last else ('q_a', 'k_a', 'v_a', 'gates_a', 'k_b', 'v_b')
        pc = _proj(hc, w_in[l], ctx_parts)
        hc_a, st_f, st_b = _mlstm_bidir(_mlstm_inputs(pc, b_gates[l], None),
                                        (_zero_state(bn), _zero_state(bn)), not last)
        hx_a, _, _ = _mlstm_bidir(_mlstm_inputs(px, b_gates[l], pos), (st_f, st_b), True)
        kc_b = _qk_norm(pc['k_b'], kn_g[l])
        vc_b = pc['v_b'].reshape(bn, L_ctx, NH_B, DH_B)
        qx_b = _qk_norm(px['q_b'], qn_g[l]) * DH_B ** -0.5
        kx_b = _qk_norm(px['k_b'], kn_g[l])
        vx_b = px['v_b'].reshape(bn, S, NH_B, DH_B)
        na_x = _neighbourhood_attention(qx_b, kx_b, vx_b, kc_b, vc_b, rpb[l], rows)
        mix_x = _merge(hx_a, px['o_a'], na_x, px['merge'], mlstm_norm_g[l],
                       w_branch_a[l], w_branch_b[l], w_out[l])
        x_new = x + g1[:, None] * mix_x
        hx2 = _rmsnorm(x_new, norm2_g[l]) * (1.0 + sc2[:, None]) + sh2[:, None]
        x_new = x_new + g2[:, None] * _swiglu(hx2, w_ffn_gate[l], w_ffn_up[l], w_ffn_down[l])
        if not last:
            qc_b = _qk_norm(pc['q_b'], qn_g[l]) * DH_B ** -0.5
            na_c = _ctx_attention(qc_b, kc_b, vc_b)
            mix_c = _merge(hc_a, pc['o_a'], na_c, pc['merge'], mlstm_norm_g[l],
                           w_branch_a[l], w_branch_b[l], w_out[l])
            ctx = ctx + g1c * mix_c
            hc2 = _rmsnorm(ctx, norm2_g[l]) * (1.0 + sc2c) + sh2c
            ctx = ctx + g2c * _swiglu(hc2, w_ffn_gate[l], w_ffn_up[l], w_ffn_down[l])
        x = x_new
    return x
```

```python
import contextlib
import numpy as np
import concourse.bass as bass
import concourse.mybir as mybir
from concourse.bass_utils import run_bass_kernel_spmd

F32 = mybir.dt.float32
BF16 = mybir.dt.bfloat16
AF = mybir.ActivationFunctionType
ALU = mybir.AluOpType
AX = mybir.AxisListType

COMPUTE = ('tensor', 'vector', 'scalar', 'gpsimd')
QUEUES = ('sync', 'gpsimd')

D = 1024
SEQ = 8192
OWN = 4096
NT = 66
EPS = 1e-6
DFF = 2816
NFF = 22


class Sched:
    def __init__(self, nc, n_dma_sems=16):
        self.nc = nc
        self.engs = ('tensor', 'vector', 'scalar', 'gpsimd', 'sync')
        self.ops = {e: [] for e in self.engs}
        self.count = {e: 0 for e in COMPUTE}
        self.sem = {e: nc.alloc_semaphore(name="sem_" + e) for e in COMPUTE}
        self.dsem = {q: [nc.alloc_semaphore(name="dsem_%s_%d" % (q, i)) for i in range(n_dma_sems)]
                     for q in QUEUES}
        self.dcount = {q: [0] * n_dma_sems for q in QUEUES}
        self.drr = {q: 0 for q in QUEUES}
        self.waited = {e: {} for e in self.engs}
        self.res = {}
        self.nops = 0

    def _semobj(self, key):
        if key[0] == 'eng':
            return self.sem[key[1]]
        return self.dsem[key[1]][key[2]]

    def _deps(self, eng, reads, writes):
        deps = {}

        def add(tok):
            if tok is None:
                return
            k, v = tok
            if deps.get(k, 0) < v:
                deps[k] = v
        for r in reads:
            st = self.res.get(r)
            if st is not None:
                add(st['w'])
        for w in writes:
            st = self.res.get(w)
            if st is not None:
                add(st['w'])
                for t in st['r']:
                    add(t)
        out = []
        wd = self.waited[eng]
        for k, v in deps.items():
            if k == ('eng', 'tensor') and eng == 'tensor':
                continue
            if wd.get(k, 0) >= v:
                continue
            wd[k] = v
            out.append((k, v))
        return out

    def _commit(self, tok, reads, writes):
        for r in reads:
            st = self.res.setdefault(r, {'w': None, 'r': []})
            st['r'].append(tok)
            if len(st['r']) > 32:
                best = {}
                for k, v in st['r']:
                    if best.get(k, 0) < v:
                        best[k] = v
                st['r'] = list(best.items())
        for w in writes:
            self.res[w] = {'w': tok, 'r': []}

    def op(self, eng, fn, reads=(), writes=()):
        waits = self._deps(eng, reads, writes)
        self.count[eng] += 1
        tok = (('eng', eng), self.count[eng])
        self.ops[eng].append((waits, fn, self.sem[eng], 1))
        self._commit(tok, reads, writes)
        self.nops += 1
        return tok

    def dma(self, q, fn, reads=(), writes=()):
        waits = self._deps(q, reads, writes)
        i = self.drr[q]
        self.drr[q] = (i + 1) % len(self.dsem[q])
        key = ('dma', q, i)
        prev = self.dcount[q][i]
        if prev > 0 and self.waited[q].get(key, 0) < prev:
            self.waited[q][key] = prev
            waits.append((key, prev))
        self.dcount[q][i] = prev + 16
        tok = (key, prev + 16)
        self.ops[q].append((waits, fn, self.dsem[q][i], 16))
        self._commit(tok, reads, writes)
        self.nops += 1
        return tok

    def T(self, name, reads, writes, *a, **kw):
        return self.op('tensor', lambda e: getattr(e, name)(*a, **kw), reads, writes)

    def V(self, name, reads, writes, *a, **kw):
        return self.op('vector', lambda e: getattr(e, name)(*a, **kw), reads, writes)

    def A(self, name, reads, writes, *a, **kw):
        return self.op('scalar', lambda e: getattr(e, name)(*a, **kw), reads, writes)

    def G(self, name, reads, writes, *a, **kw):
        return self.op('gpsimd', lambda e: getattr(e, name)(*a, **kw), reads, writes)

    def DMA(self, q, reads, writes, **kw):
        return self.dma(q, lambda e: e.dma_start(**kw), reads, writes)

    def barrier(self, skip_queues=()):
        toks = [(('eng', e), self.count[e]) for e in COMPUTE if self.count[e] > 0]
        for q in QUEUES:
            if q in skip_queues:
                continue
            for i, c in enumerate(self.dcount[q]):
                if c > 0:
                    toks.append((('dma', q, i), c))
        for e in self.engs:
            waits = []
            for k, v in toks:
                if k == ('eng', e):
                    continue
                if self.waited[e].get(k, 0) < v:
                    self.waited[e][k] = v
                    waits.append((k, v))
            if waits:
                self.ops[e].append((waits, None, None, 0))

    def check(self):
        val = {}
        pos = {e: 0 for e in self.engs}
        prog = True
        while prog:
            prog = False
            for e in self.engs:
                q = self.ops[e]
                while pos[e] < len(q):
                    waits, fn, sem, inc = q[pos[e]]
                    if any(val.get(k, 0) < v for k, v in waits):
                        break
                    if fn is not None:
                        k = None
                        for kk in ([('eng', x) for x in COMPUTE] + [('dma', qq, i) for qq in QUEUES for i in range(len(self.dsem[qq]))]):
                            if self._semobj(kk) is sem:
                                k = kk
                                break
                        val[k] = val.get(k, 0) + inc
                    pos[e] += 1
                    prog = True
        stuck = {e: (pos[e], len(self.ops[e])) for e in self.engs if pos[e] < len(self.ops[e])}
        if stuck:
            msg = []
            for e in stuck:
                waits = self.ops[e][pos[e]][0]
                msg.append("%s at %d/%d waits %s" % (e, pos[e], len(self.ops[e]), [(k, v, val.get(k, 0)) for k, v in waits if val.get(k, 0) < v]))
            raise RuntimeError("DEADLOCK: " + "; ".join(msg))

    def emit(self):
        nc = self.nc
        self.barrier()
        self.check()
        with nc.Block() as block:
            def mk(e):
                def body(eng):
                    for waits, fn, sem, inc in self.ops[e]:
                        for k, v in waits:
                            eng.wait_ge(self._semobj(k), v)
                        if fn is not None:
                            fn(eng).then_inc(sem, inc)
                return body
            block.tensor(mk('tensor'))
            block.vector(mk('vector'))
            block.scalar(mk('scalar'))
            block.gpsimd(mk('gpsimd'))
            block.sync(mk('sync'))


def build_nc(debug=False, phases=(0, 1, 2, 3, 4, 5, 6), nblk=32, dump=True, scr_in=()):
    nc = bass.Bass("TRN2", target_bir_lowering=False)
    S = Sched(nc)
    es = contextlib.ExitStack()

    def din(name, shape, dt=F32):
        return nc.dram_tensor(name, list(shape), dt, kind="ExternalInput").ap()

    def dscr(name, shape, dt):
        if name in scr_in:
            return nc.dram_tensor(name, list(shape), dt, kind="ExternalInput").ap()
        if debug:
            return nc.dram_tensor(name, list(shape), dt, kind="ExternalOutput").ap()
        return nc.dram_tensor(name, list(shape), dt).ap()

    x = din("x", [SEQ, D])
    ctx = din("ctx", [256, D])
    cvec = din("cvec", [128, 8, 2])
    w_mod = din("w_mod", [D, 6 * D])
    vecs = din("vecs", [128, 48 + 8 + 8])
    w_in = din("w_in", [D, 8224])
    rowv = din("rowv", [32 + 1024 + 64 + 64])
    rope = din("rope", [NT * 128, 128])
    consts = din("consts", [128, 4 * 128])
    biasT = din("biasT", [128, 4 * 16 * 128])
    maskT = din("maskT", [32, 128, 512])
    w_a = din("w_a", [D, D])
    w_b = din("w_b", [D, D])
    w_o = din("w_o", [D, D])
    w_g = din("w_g", [D, DFF])
    w_u = din("w_u", [D, DFF])
    w_d = din("w_d", [DFF, D])
    y = nc.dram_tensor("y", [OWN, D], F32, kind="ExternalOutput").ap()

    QA = dscr("QA", [32, 128, 512], BF16)
    KA = dscr("KA", [NT, 128, 512], BF16)
    VA = dscr("VA", [NT, 128, 8 * 129], BF16)
    QB = dscr("QB", [64, 64, 1024], BF16)
    KB = dscr("KB", [72, 80, 1024], BF16)
    VB = dscr("VB", [72, 80, 1040], BF16)
    KCB = dscr("KCB", [256, 1024], BF16)
    VCB = dscr("VCB", [256, 1040], BF16)
    HF = dscr("HF", [32, 128, 1024], F32)
    HS = dscr("HS", [32, 128, 1024], F32)
    NA = dscr("NA", [64, 64, 1024], BF16)
    XNT = dscr("XNT", [8, 128, OWN], F32)
    GDBG = dscr("GDBG", [128, 6 * 2 * 8 * NT], F32) if debug else None

    sb = lambda name, shape, dt=F32: es.enter_context(nc.sbuf_tensor(name, list(shape), dt))

    cst = sb("cst", [128, 4, 128])
    identb = sb("identb", [128, 128], BF16)
    onesb = sb("onesb", [128, 128], BF16)
    MV = sb("MV", [128, 8, 8])
    mhalf = sb("mhalf", [128, 1])
    epsb = sb("epsb", [128, 1])
    mid = contextlib.ExitStack()
    sbm = lambda name, shape, dt=F32: mid.enter_context(nc.sbuf_tensor(name, list(shape), dt))
    GT = sbm("GT", [128, 32, NT])
    TAB = sbm("TAB", [128, 6, 8, NT])
    WA = sbm("WA", [128, 8, 5152], BF16)
    wsrc = [(0, 0, 1024), (1024, 1024, 1024), (2048, 3072, 32), (2080, 3104, 1024), (3104, 4128, 1024), (4128, 5152, 1024)]
    for (dc, sc_, n) in wsrc:
        for kc in range(8):
            S.DMA('gpsimd', [], [('WA', dc, kc)], out=WA[:, kc, dc:dc + n], in_=w_in[kc * 128:(kc + 1) * 128, sc_:sc_ + n])
    WAkeys = [('WA', dc, kc) for (dc, _, _) in wsrc for kc in range(8)]
    zt = sbm("zt", [128, 8320], BF16)
    S.G('memset', [], ['zt'], zt[:], 0.0)
    S.DMA('sync', [], ['cst'], out=cst[:], in_=consts.rearrange("p (a b) -> p a b", a=4))
    S.V('tensor_copy', ['cst'], ['identb'], out=identb[:], in_=cst[:, 0, :])
    S.V('tensor_copy', ['cst'], ['onesb'], out=onesb[:], in_=cst[:, 3, :])
    S.G('memset', [], ['mhalf'], mhalf[:], -0.5)
    S.G('memset', [], ['epsb'], epsb[:], EPS)
    ident = cst[:, 0, :]


    cv = sbm("cv", [128, 8, 2])
    sv = sbm("sv", [128, 8, 2])
    vt = sbm("vt", [128, 64])
    modT = sbm("modT", [128, 48, 2])
    tmp8 = sbm("tmp8", [128, 8])
    n1g = vt[:, 48:56]
    n2g = vt[:, 56:64]

    def mod_blocks(blks, wm, pm, epi=True):
        for blk in blks:
            sl = blk % 2
            S.DMA('sync', [], [('wm', sl)], out=wm[:, sl, :, :],
                  in_=w_mod[:, blk * 512:(blk + 1) * 512].rearrange("(k p) n -> p k n", p=128))
            for jj in range(4):
                j = blk * 4 + jj
                for kc in range(8):
                    S.T('matmul', [('wm', sl), 'sv'], ['pm'], pm[:, j, :], lhsT=wm[:, sl, kc, jj * 128:(jj + 1) * 128],
                        rhs=sv[:, kc, :], start=(kc == 0), stop=(kc == 7))
        if not epi:
            return
        j0, j1 = blks[0] * 4, blks[-1] * 4 + 4
        S.V('tensor_tensor', ['pm', 'vt'], ['modT'], out=modT[:, j0:j1, :], in0=pm[:, j0:j1, :],
            in1=vt[:, j0:j1].unsqueeze(2).to_broadcast([128, j1 - j0, 2]), op=ALU.add)

    def mod_scale(dst, src, gg):
        S.V('tensor_scalar', ['modT'], ['tmp8'], out=tmp8[:], in0=src, scalar1=1.0, scalar2=None, op0=ALU.add)
        S.V('tensor_tensor', ['tmp8', 'vt'], ['MV'], out=MV[:, dst, :], in0=tmp8[:], in1=gg, op=ALU.mult)

    if 0 in phases:
        with contextlib.ExitStack() as ps:
            wm = ps.enter_context(nc.sbuf_tensor("wm", [128, 2, 8, 512], F32))
            pm = ps.enter_context(nc.psum_tensor("pm", [128, 48, 2], F32))
            S.DMA('sync', [], ['cv'], out=cv[:], in_=cvec)
            S.DMA('sync', [], ['vt'], out=vt[:], in_=vecs)
            S.A('activation', ['cv'], ['sv'], out=sv[:], in_=cv[:], func=AF.Silu)
            mod_blocks([0, 1, 2, 3], wm, pm)
            mod_scale(0, modT[:, 8:16, 0], n1g)
            mod_scale(2, modT[:, 8:16, 1], n1g)
            S.V('tensor_copy', ['modT'], ['MV'], out=MV[:, 1, :], in_=modT[:, 0:8, 0])
            S.V('tensor_copy', ['modT'], ['MV'], out=MV[:, 3, :], in_=modT[:, 0:8, 1])
        S.barrier(skip_queues=('gpsimd',))

    if 1 in phases:
        with contextlib.ExitStack() as ps:
            sb1 = lambda name, shape, dt=F32: ps.enter_context(nc.sbuf_tensor(name, list(shape), dt))
            xt = sb1("xt", [128, 4, 1024])
            junk = sb1("junk", [128, 1024], BF16)
            xs = sb1("xs", [128, 3, 1024], BF16)
            hxT = sb1("hxT", [128, 2, 8, 128], BF16)
            st1 = sb1("st1", [128, 3, 4])
            rp = sb1("rp", [128, 4, 128])
            bg = sb1("bg", [128, 32])
            gq = sb1("gq", [128, 64])
            gk = sb1("gk", [128, 64])
            QAo = sb1("QAo", [128, 2, 512], BF16)
            KAo = sb1("KAo", [128, 2, 512], BF16)
            VAo = sb1("VAo", [128, 2, 8, 129], BF16)
            QBo = sb1("QBo", [128, 2, 1024], BF16)
            KBo = sb1("KBo", [128, 2, 1024], BF16)
            VBo = sb1("VBo", [128, 2, 16, 65], BF16)
            rA = sb1("rA", [128, 4, 256])
            sq = sb1("sq", [128, 2, 512])
            tmpn = sb1("tmpn", [128, 2, 512])
            st8 = sb1("st8", [128, 2, 3, 8])
            pT = ps.enter_context(nc.psum_tensor("pT1", [128, 2, 8, 128], BF16))
            pP = [ps.enter_context(nc.psum_tensor("pP%d" % i, [128, 512], F32)) for i in range(6)]

            S.DMA('sync', [], ['bg'], out=bg[:], in_=rowv[0:32].partition_broadcast(128))
            S.DMA('sync', [], ['gq'], out=gq[:], in_=rowv[1056:1120].partition_broadcast(128))
            S.DMA('sync', [], ['gk'], out=gk[:], in_=rowv[1120:1184].partition_broadcast(128))
            S.V('tensor_scalar', ['gq'], ['gq'], out=gq[:], in0=gq[:], scalar1=0.125, scalar2=None, op0=ALU.mult)
            S.G('memset', [], [('VAo', 0), ('VAo', 1)], VAo[:, :, :, 128:129], 1.0)
            S.G('memset', [], [('VBo', 0), ('VBo', 1)], VBo[:, :, :, 64:65], 1.0)

            ppi = [0]

            def proj(sl, c0, n, reads_extra=()):
                i = ppi[0] % 6
                ppi[0] += 1
                dcs = [dc for (dc, _, nn) in wsrc if dc < c0 + n and c0 < dc + nn]
                for kc in range(8):
                    S.T('matmul', [('hxT', sl)] + [('WA', dc, kc) for dc in dcs], [('pP', i)], pP[i][:, 0:n], lhsT=hxT[:, sl, kc, :],
                        rhs=WA[:, kc, c0:c0 + n], start=(kc == 0), stop=(kc == 7))
                return pP[i], ('pP', i)

            def rope_epi(pt, pk, sl, out_tile, okey, tab_off, s3=0):
                v = pt[:, 0:512].rearrange("p (h a f i) -> p h a f i", h=8, a=2, f=2, i=16)
                x1 = v[:, :, :, 0, :]
                x2 = v[:, :, :, 1, :]
                cosb = rp[:, s3, tab_off:tab_off + 32].rearrange("p (a i) -> p a i", a=2).unsqueeze(1).to_broadcast([128, 8, 2, 16])
                sinb = rp[:, s3, tab_off + 32:tab_off + 64].rearrange("p (a i) -> p a i", a=2).unsqueeze(1).to_broadcast([128, 8, 2, 16])
                tv = [rA[:, i, :].rearrange("p (h a i) -> p h a i", h=8, a=2) for i in range(4)]
                S.V('tensor_tensor', [pk, ('rp', s3)], [('rA', 0)], out=tv[0], in0=x1, in1=cosb, op=ALU.mult)
                S.V('tensor_tensor', [pk, ('rp', s3)], [('rA', 1)], out=tv[1], in0=x2, in1=sinb, op=ALU.mult)
                S.V('tensor_tensor', [pk, ('rp', s3)], [('rA', 2)], out=tv[2], in0=x2, in1=cosb, op=ALU.mult)
                S.V('tensor_tensor', [pk, ('rp', s3)], [('rA', 3)], out=tv[3], in0=x1, in1=sinb, op=ALU.mult)
                ov = out_tile.rearrange("p (h a f i) -> p h a f i", h=8, a=2, f=2, i=16)
                S.G('tensor_tensor', [('rA', 0), ('rA', 1)], [okey], out=ov[:, :, :, 0, :], in0=tv[0], in1=tv[1], op=ALU.subtract)
                S.G('tensor_tensor', [('rA', 2), ('rA', 3)], [okey], out=ov[:, :, :, 1, :], in0=tv[2], in1=tv[3], op=ALU.add)

            def qknorm_epi(pt, pk, sl, g_tile, gkey, out_ap, okey):
                s2 = ppi[0] % 2
                S.A('activation', [pk], [('sq', s2)], out=sq[:, s2, :], in_=pt[:, 0:512], func=AF.Square)
                S.V('tensor_reduce', [('sq', s2)], [('st8', s2)], out=st8[:, s2, 0, :], in_=sq[:, s2, :].rearrange("p (h d) -> p h d", h=8),
                    axis=AX.X, op=ALU.add)
                S.V('tensor_scalar', [('st8', s2)], [('st8', s2)], out=st8[:, s2, 1, :], in0=st8[:, s2, 0, :], scalar1=1.0 / 64, scalar2=EPS,
                    op0=ALU.mult, op1=ALU.add)
                S.G('tensor_tensor', [('st8', s2), 'mhalf'], [('st8', s2)], out=st8[:, s2, 2, :], in0=st8[:, s2, 1, :],
                    in1=mhalf[:, 0:1].to_broadcast([128, 8]), op=ALU.pow)
                S.V('tensor_tensor', [pk, ('st8', s2)], [('tmpn', s2)], out=tmpn[:, s2, :].rearrange("p (h d) -> p h d", h=8),
                    in0=pt[:, 0:512].rearrange("p (h d) -> p h d", h=8),
                    in1=st8[:, s2, 2, :].unsqueeze(2).to_broadcast([128, 8, 64]), op=ALU.mult)
                S.G('tensor_tensor', [('tmpn', s2), gkey], [okey], out=out_ap.rearrange("p (h d) -> p h d", h=8),
                    in0=tmpn[:, s2, :].rearrange("p (h d) -> p h d", h=8),
                    in1=g_tile[:, 0:64].unsqueeze(1).to_broadcast([128, 8, 64]), op=ALU.mult)

            ORD = list(range(36, NT)) + [34, 35, 0, 1] + list(range(2, 34))
            POS = {T_: p_ for p_, T_ in enumerate(ORD)}

            def prepA0(T):
                s4 = POS[T] % 4
                src = ctx[T * 128:(T + 1) * 128, :] if T < 2 else x[(T - 2) * 128:(T - 1) * 128, :]
                S.DMA('sync', [], [('xt', s4)], out=xt[:, s4, :], in_=src)
                S.DMA('sync', [], [('rp', s4)], out=rp[:, s4, :], in_=rope[T * 128:(T + 1) * 128, :])

            def prepA(T):
                sl = POS[T] % 3
                s4 = POS[T] % 4
                S.A('activation', [('xt', s4)], ['junk', ('st1', sl)], out=junk[:], in_=xt[:, s4, :], func=AF.Square,
                    accum_out=st1[:, sl, 0:1])
                S.A('activation', [('st1', sl)], [('st1', sl)], out=st1[:, sl, 1:2], in_=st1[:, sl, 0:1], func=AF.Ln, scale=1.0 / D, bias=epsb[:, 0:1])
                S.A('activation', [('st1', sl)], [('st1', sl)], out=st1[:, sl, 2:3], in_=st1[:, sl, 1:2], func=AF.Exp, scale=-0.5)
                if 2 <= T < 34:
                    S.A('activation', [('xt', s4), ('st1', sl)], [('xs', sl)], out=xs[:, sl, :], in_=xt[:, s4, :], func=AF.Copy,
                        scale=st1[:, sl, 2:3])
                else:
                    S.V('tensor_scalar', [('xt', s4), ('st1', sl)], [('xs', sl)], out=xs[:, sl, :], in0=xt[:, s4, :], scalar1=st1[:, sl, 2:3],
                        scalar2=None, op0=ALU.mult)

            def prepB(T):
                sl = POS[T] % 2
                s3 = POS[T] % 3
                for kc in range(8):
                    S.T('transpose', [('xs', s3), 'identb'], [('pT', sl)], out=pT[:, sl, kc, :], in_=xs[:, s3, kc * 128:(kc + 1) * 128],
                        identity=identb[:])
                gi, si = (2, 3) if T < 2 else (0, 1)
                for kc in range(8):
                    if (2 <= T < 34) or kc % 2 == 0:
                        S.A('activation', [('pT', sl), 'MV'], [('hxT', sl)], out=hxT[:, sl, kc, :], in_=pT[:, sl, kc, :], func=AF.Identity,
                            scale=MV[:, gi, kc:kc + 1], bias=MV[:, si, kc:kc + 1])
                    else:
                        S.V('tensor_scalar', [('pT', sl), 'MV'], [('hxT', sl)], out=hxT[:, sl, kc, :], in0=pT[:, sl, kc, :],
                            scalar1=MV[:, gi, kc:kc + 1], scalar2=MV[:, si, kc:kc + 1], op0=ALU.mult, op1=ALU.add)

            def groups(T):
                sl = POS[T] % 2
                is_ctx = T < 2
                is_own = 2 <= T < 34
                halo = T in (34, 35)
                G = []

                def g_qa():
                    pt, pk = proj(sl, 0, 512)
                    rope_epi(pt, pk, sl, QAo[:, sl, :], ('QAo', sl), 64, POS[T] % 4)
                    S.DMA('sync', [('QAo', sl)], [('QA', T)], out=QA[T - 2], in_=QAo[:, sl, :])

                def g_ka():
                    pt, pk = proj(sl, 512, 512)
                    rope_epi(pt, pk, sl, KAo[:, sl, :], ('KAo', sl), 0, POS[T] % 4)
                    S.DMA('sync', [('KAo', sl)], [('KA', T)], out=KA[T], in_=KAo[:, sl, :])

                def g_va(hf):
                    def f():
                        pt, pk = proj(sl, 1024 + hf * 512, 512)
                        S.A('activation', [pk], [('VAo', sl)], out=VAo[:, sl, hf * 4:(hf + 1) * 4, 0:128],
                            in_=pt[:, 0:512].rearrange("p (h e) -> p h e", h=4), func=AF.Copy)
                        if hf == 1:
                            S.DMA('sync', [('VAo', sl)], [('VA', T)], out=VA[T], in_=VAo[:, sl, :, :].rearrange("p h e -> p (h e)"))
                    return f

                def g_gates():
                    pt, pk = proj(sl, 2048, 32)
                    S.V('tensor_tensor', [pk, 'bg'], ['GT'], out=GT[:, :, T], in0=pt[:, 0:32], in1=bg[:], op=ALU.add)

                def g_qb(hf):
                    def f():
                        pt, pk = proj(sl, 2080 + hf * 512, 512)
                        qknorm_epi(pt, pk, sl, gq, 'gq', QBo[:, sl, hf * 512:(hf + 1) * 512], ('QBo', sl))
                        if hf == 1:
                            r0 = (T - 2) * 2
                            for rr in range(2):
                                S.DMA('sync', [('QBo', sl)], [('QB', T)], out=QB[r0 + rr], in_=QBo[rr * 64:(rr + 1) * 64, sl, :])
                    return f

                def g_kb(hf):
                    def f():
                        pt, pk = proj(sl, 3104 + hf * 512, 512)
                        qknorm_epi(pt, pk, sl, gk, 'gk', KBo[:, sl, hf * 512:(hf + 1) * 512], ('KBo', sl))
                        if hf == 1:
                            if is_ctx:
                                S.DMA('sync', [('KBo', sl)], [('KCB', T)], out=KCB[T * 128:(T + 1) * 128, :], in_=KBo[:, sl, :])
                            else:
                                r0 = (T - 2) * 2 + 4
                                for rr in range(2):
                                    S.DMA('sync', [('KBo', sl)], [('KBs', T)], out=KB[r0 + rr, 8:72, :], in_=KBo[rr * 64:(rr + 1) * 64, sl, :])
                    return f

                def g_vb(hf):
                    def f():
                        pt, pk = proj(sl, 4128 + hf * 512, 512)
                        S.A('activation', [pk], [('VBo', sl)], out=VBo[:, sl, hf * 8:(hf + 1) * 8, 0:64],
                            in_=pt[:, 0:512].rearrange("p (h e) -> p h e", h=8), func=AF.Copy)
                        if hf == 1:
                            if is_ctx:
                                S.DMA('sync', [('VBo', sl)], [('VCB', T)], out=VCB[T * 128:(T + 1) * 128, :],
                                      in_=VBo[:, sl, :, :].rearrange("p h e -> p (h e)"))
                            else:
                                r0 = (T - 2) * 2 + 4
                                for rr in range(2):
                                    S.DMA('sync', [('VBo', sl)], [('VBs', T)], out=VB[r0 + rr, 8:72, :],
                                          in_=VBo[rr * 64:(rr + 1) * 64, sl, :, :].rearrange("p h e -> p (h e)"))
                    return f

                if is_own:
                    G.append(g_qa)
                G += [g_ka, g_va(0), g_va(1), g_gates]
                if is_own:
                    G += [g_qb(0), g_qb(1)]
                if is_own or halo or is_ctx:
                    G += [g_kb(0), g_kb(1), g_vb(0), g_vb(1)]
                return G

            for p_ in range(3):
                prepA0(ORD[p_])
            prepA(ORD[0])
            prepA(ORD[1])
            prepB(ORD[0])
            for p_ in range(NT):
                G = groups(ORD[p_])
                a_ = max(1, len(G) // 3)
                b_ = max(2, (2 * len(G)) // 3)
                for gi_, g in enumerate(G):
                    if gi_ == 0 and p_ + 3 < NT:
                        prepA0(ORD[p_ + 3])
                    if gi_ == a_ and p_ + 2 < NT:
                        prepA(ORD[p_ + 2])
                    if gi_ == b_ and p_ + 1 < NT:
                        prepB(ORD[p_ + 1])
                    g()
        S.barrier()


    Tf = [0, 1] + list(range(2, 34))
    Tb = [1, 0] + list(range(65, 33, -1)) + list(range(33, 1, -1))

    if 2 in phases:
        if 1 not in phases:
            S.V('memset', [], ['GT'], GT[:], 0.0)
        with contextlib.ExitStack() as ps:
            sb2 = lambda name, shape, dt=F32: ps.enter_context(nc.sbuf_tensor(name, list(shape), dt))
            TH = sb2("TH", [128, 32, NT])
            IG = sb2("IG", [128, 2, 8, NT])
            LF = sb2("LF", [128, 2, 8, NT])
            Bc = sb2("Bc", [128, 2, 8, NT])
            U = sb2("U", [128, 2, 8, NT])
            MB = sb2("MB", [128, 2, 8, NT])
            tmpg = sb2("tmpg", [128, 2, 8, NT])
            UmT = sb2("UmT", [128, 2, 8])
            TotT = sb2("TotT", [128, 2, 8])
            Um = sb2("Um", [8, 2, NT])
            Tot = sb2("Tot", [8, 2, NT])
            Mx = sb2("Mx", [8, 2, NT])
            MPT = sb2("MPT", [8, 2, NT])
            Dl = sb2("Dl", [8, 2, NT])
            Dg = sb2("Dg", [8, 2, 8, NT])
            pb = ps.enter_context(nc.psum_tensor("pb", [128, 4, 512], F32))
            ptr = [ps.enter_context(nc.psum_tensor("ptr%d" % i, [128, 512], F32)) for i in range(2)]
            psm = ps.enter_context(nc.psum_tensor("psm", [128, 512], F32))
            HN = 4 * NT
            if 0 in phases:
                wm2 = ps.enter_context(nc.sbuf_tensor("wm2", [128, 2, 8, 512], F32))
                pm2 = ps.enter_context(nc.psum_tensor("pm2", [128, 48, 2], F32))
            S.A('activation', ['GT'], ['TH'], out=TH[:], in_=GT[:], func=AF.Tanh, scale=1.0 / 15)
            for d in range(2):
                S.V('tensor_scalar', ['TH'], ['IG'], out=IG[:, d], in0=TH[:, 16 * d:16 * d + 8, :], scalar1=15.0, scalar2=None, op0=ALU.mult)
                S.A('activation', ['TH'], ['tmpg'], out=tmpg[:, d], in_=TH[:, 16 * d + 8:16 * d + 16, :], func=AF.Exp, scale=-15.0)
            S.A('activation', ['tmpg'], ['tmpg'], out=tmpg[:], in_=tmpg[:], func=AF.Ln, bias=1.0)
            S.V('tensor_scalar', ['tmpg'], ['LF'], out=LF[:], in0=tmpg[:], scalar1=-1.0, scalar2=None, op0=ALU.mult)
            for d in range(2):
                lf2 = LF[:, d].rearrange("p h t -> p (h t)")
                for hf in range(2):
                    S.T('matmul', ['LF', 'cst'], ['pb'], pb[:, 2 * d + hf, 0:HN], lhsT=cst[:, 1 + d, :], rhs=lf2[:, hf * HN:(hf + 1) * HN],
                        start=True, stop=True)
                    S.V('tensor_copy', ['pb'], ['Bc'], out=Bc[:, d].rearrange("p h t -> p (h t)")[:, hf * HN:(hf + 1) * HN],
                        in_=pb[:, 2 * d + hf, 0:HN])
            S.V('tensor_tensor', ['IG', 'Bc'], ['U'], out=U[:], in0=IG[:], in1=Bc[:], op=ALU.subtract)
            k = 0
            for d in range(2):
                for h in range(8):
                    ki = k % 2
                    pt_ = ptr[ki]
                    k += 1
                    S.T('transpose', ['U', 'cst'], [('ptr', ki)], out=pt_[0:NT, 0:128], in_=U[:, d, h, :], identity=ident)
                    S.V('tensor_reduce', [('ptr', ki)], ['UmT'], out=UmT[0:NT, d, h:h + 1], in_=pt_[0:NT, 0:128], axis=AX.X, op=ALU.max)
                    S.T('matmul', ['LF', 'cst'], ['psm'], psm[0:NT, 2 * (8 * d + h):2 * (8 * d + h) + 2], lhsT=LF[:, d, h, :], rhs=cst[:, 3, 0:2],
                        start=True, stop=True)
            S.V('tensor_copy', ['psm'], ['TotT'], out=TotT[0:NT].rearrange("p d h -> p (d h)"),
                in_=psm[0:NT, 0:32].rearrange("p (k two) -> p k two", two=2)[:, :, 0])
            for d in range(2):
                S.T('transpose', ['UmT', 'cst'], [('ptr', 0)], out=ptr[0][0:8, 0:NT], in_=UmT[0:NT, d, :], identity=ident[0:NT, 0:NT])
                S.V('tensor_copy', [('ptr', 0)], ['Um'], out=Um[:, d, :], in_=ptr[0][0:8, 0:NT])
                S.T('transpose', ['TotT', 'cst'], [('ptr', 1)], out=ptr[1][0:8, 0:NT], in_=TotT[0:NT, d, :], identity=ident[0:NT, 0:NT])
                S.V('tensor_copy', [('ptr', 1)], ['Tot'], out=Tot[:, d, :], in_=ptr[1][0:8, 0:NT])
            if 0 in phases:
                mod_blocks(list(range(4, 12)), wm2, pm2, epi=False)
            padkeys = []
            for (dst, w) in ((KB, 1024), (VB, 1040)):
                top = dst[0:4].rearrange("r c f -> (r c) f")
                for (a, b) in ((0, 128), (128, 256), (256, 320)):
                    padkeys.append(('pad', len(padkeys)))
                    S.DMA('sync', ['zt'], [padkeys[-1]], out=top[a:b, :], in_=zt[0:b - a, 0:w])
                padkeys.append(('pad', len(padkeys)))
                S.DMA('sync', ['zt'], [padkeys[-1]], out=dst[4:72, 0:8, :], in_=zt[0:68, 0:8 * w].rearrange("p (c f) -> p c f", c=8))
                padkeys.append(('pad', len(padkeys)))
                S.DMA('sync', ['zt'], [padkeys[-1]], out=dst[4:72, 72:80, :], in_=zt[0:68, 0:8 * w].rearrange("p (c f) -> p c f", c=8))

            S.V('memset', [], ['MPT'], MPT[:], 0.0)
            orders = (Tf, Tb)
            for step in range(NT):
                for d in range(2):
                    od = orders[d]
                    if step >= len(od):
                        continue
                    T = od[step]
                    S.V('tensor_tensor', ['MPT', 'Um'], ['Mx'], out=Mx[:, d, T:T + 1], in0=MPT[:, d, T:T + 1], in1=Um[:, d, T:T + 1], op=ALU.max)
                    if step + 1 < len(od):
                        Tn = od[step + 1]
                        S.V('tensor_tensor', ['Mx', 'Tot'], ['MPT'], out=MPT[:, d, Tn:Tn + 1], in0=Mx[:, d, T:T + 1], in1=Tot[:, d, T:T + 1], op=ALU.add)
            S.V('tensor_copy', ['Mx'], ['Mx'], out=Mx[:, 0, 34:NT], in_=Mx[:, 1, 34:NT])
            S.V('tensor_tensor', ['MPT', 'Mx'], ['Dl'], out=Dl[:], in0=MPT[:], in1=Mx[:], op=ALU.subtract)
            S.V('tensor_scalar', ['Dl'], ['Dl'], out=Dl[:], in0=Dl[:], scalar1=0.0, scalar2=None, op0=ALU.min)
            S.A('activation', ['Dl'], ['Dl'], out=Dl[:], in_=Dl[:], func=AF.Exp)
            eye8 = cst[0:8, 0, 0:8]
            for (srcv, skey, which) in ((Mx, 'Mx', 0), (Dl, 'Dl', 1)):
                for d in range(2):
                    S.V('tensor_tensor', [skey, 'cst'], ['Dg'], out=Dg[:, d], in0=srcv[:, d, :].unsqueeze(1).to_broadcast([8, 8, NT]),
                        in1=eye8.unsqueeze(2).to_broadcast([8, 8, NT]), op=ALU.mult)
                for d in range(2):
                    dg2 = Dg[:, d].rearrange("p h t -> p (h t)")
                    for hf in range(2):
                        S.T('matmul', ['Dg', 'cst'], ['pb'], pb[:, 2 * d + hf, 0:HN], lhsT=cst[0:8, 3, :], rhs=dg2[:, hf * HN:(hf + 1) * HN],
                            start=True, stop=True)
                        dstv = MB[:, d] if which == 0 else TAB[:, 4 + d]
                        S.V('tensor_copy', ['pb'], ['MB' if which == 0 else 'TAB'], out=dstv.rearrange("p h t -> p (h t)")[:, hf * HN:(hf + 1) * HN],
                            in_=pb[:, 2 * d + hf, 0:HN])
            S.V('tensor_tensor', ['U', 'MB'], ['tmpg'], out=tmpg[:], in0=U[:], in1=MB[:], op=ALU.subtract)
            S.A('activation', ['tmpg'], ['TAB'], out=TAB[:, 0:2], in_=tmpg[:], func=AF.Exp)
            S.V('tensor_tensor', ['Bc', 'MB'], ['tmpg'], out=tmpg[:], in0=Bc[:], in1=MB[:], op=ALU.add)
            S.A('activation', ['tmpg'], ['TAB'], out=TAB[:, 2:4], in_=tmpg[:], func=AF.Exp, scale=-1.0)
            if 0 in phases:
                S.V('tensor_tensor', ['pm', 'vt'], ['modT'], out=modT[:, 16:48, :], in0=pm2[:, 16:48, :],
                    in1=vt[:, 16:48].unsqueeze(2).to_broadcast([128, 32, 2]), op=ALU.add)
                mod_scale(5, modT[:, 32:40, 0], n2g)
                S.V('tensor_copy', ['modT'], ['MV'], out=MV[:, 4, :], in_=modT[:, 16:24, 0])
                S.V('tensor_copy', ['modT'], ['MV'], out=MV[:, 6, :], in_=modT[:, 24:32, 0])
                S.V('tensor_copy', ['modT'], ['MV'], out=MV[:, 7, :], in_=modT[:, 40:48, 0])
        S.barrier()

    if 3 in phases:
        with contextlib.ExitStack() as ps:
            sb3 = lambda name, shape, dt=F32: ps.enter_context(nc.sbuf_tensor(name, list(shape), dt))
            RD = 6
            KAt = sb3("KAt", [128, 3, 512], BF16)
            QAt = sb3("QAt", [128, 3, 512], BF16)
            VAt = sb3("VAt", [128, 3, 8, 129], BF16)
            kqT = sb3("kqT", [128, 3, 8, 128], BF16)
            ScT = sb3("ScT", [128, RD, 128], BF16)
            Vw = sb3("Vw", [128, RD, 129], BF16)
            C = sb3("C", [128, 8, 129])
            Cd16 = sb3("Cd16", [128, RD, 129], BF16)
            Hin = sb3("Hin", [128, 3, 1024])
            Hout = sb3("Hout", [128, 3, 1024])
            dn = sb3("dn", [128, RD, 4])
            pTk = ps.enter_context(nc.psum_tensor("pTk", [128, 2, 8, 128], BF16))
            pI = [ps.enter_context(nc.psum_tensor("pI%d" % i, [128, 512], F32)) for i in range(RD)]
            for d in range(2):
                order = (Tf, Tb)[d]
                S.V('memset', [], ['C%d' % h for h in range(8)], C[:], 0.0)
                items = [(k, h) for k in range(len(order)) for h in range(8)]

                def loads3(k):
                    T = order[k]
                    sl = k % 3
                    S.DMA('sync', [], [('KAt', sl)], out=KAt[:, sl, :], in_=KA[T])
                    S.DMA('sync', [], [('VAt', sl)], out=VAt[:, sl, :, :].rearrange("p h e -> p (h e)"), in_=VA[T])
                    if 2 <= T < 34:
                        S.DMA('sync', [], [('QAt', sl)], out=QAt[:, sl, :], in_=QA[T - 2])
                        if d == 1:
                            S.DMA('sync', [], [('Hin', sl)], out=Hin[:, sl, :], in_=HF[T - 2])

                def transp3(k):
                    T = order[k]
                    sl = k % 3
                    s2 = k % 2
                    if not (2 <= T < 34):
                        return
                    for hp in range(4):
                        S.T('transpose', [('KAt', sl), 'identb'], [('pTk', s2)], out=pTk[:, s2, hp, :],
                            in_=KAt[:, sl, hp * 128:(hp + 1) * 128], identity=identb[:])
                        S.T('transpose', [('QAt', sl), 'identb'], [('pTk', s2)], out=pTk[:, s2, 4 + hp, :],
                            in_=QAt[:, sl, hp * 128:(hp + 1) * 128], identity=identb[:])
                    S.A('activation', [('pTk', s2)], [('kqT', sl)], out=kqT[:, sl], in_=pTk[:, s2], func=AF.Copy)

                def info(i):
                    k, h = items[i]
                    T = order[k]
                    hp, po = h // 2, 64 * (h % 2)
                    return k, h, T, k % 3, i % RD, (2 <= T < 34), hp, slice(po, po + 64)

                def s0(i):
                    k, h, T, sl, r, readout, hp, pr = info(i)
                    if h == 0 and k + 1 < len(order):
                        loads3(k + 1)
                    if h == 4 and k + 1 < len(order):
                        transp3(k + 1)
                    S.G('tensor_scalar', [('VAt', sl), 'TAB'], [('Vw', r)], out=Vw[:, r, :], in0=VAt[:, sl, h, :], scalar1=TAB[:, d, h, T:T + 1],
                        scalar2=0.0, op0=ALU.mult, op1=ALU.add)

                def s1(i):
                    k, h, T, sl, r, readout, hp, pr = info(i)
                    S.T('matmul', [('KAt', sl), ('Vw', r)], [('pI', r)], pI[r][pr, 128:257], lhsT=KAt[:, sl, h * 64:(h + 1) * 64],
                        rhs=Vw[:, r, :], start=True, stop=True)
                    if readout:
                        S.T('matmul', [('kqT', sl)], [('pI', r)], pI[r][:, 0:128], lhsT=kqT[pr, sl, hp, :], rhs=kqT[pr, sl, 4 + hp, :],
                            start=True, stop=True)

                def s2(i):
                    k, h, T, sl, r, readout, hp, pr = info(i)
                    ck = 'C%d' % h
                    if readout:
                        S.V('tensor_tensor', [('pI', r), 'cst'], [('ScT', r)], out=ScT[:, r, :], in0=pI[r][:, 0:128], in1=cst[:, 1 + d, :], op=ALU.mult)
                        S.G('tensor_scalar', [ck, 'TAB'], [('Cd16', r)], out=Cd16[pr, r, :], in0=C[pr, h, :], scalar1=TAB[pr, 4 + d, h, T:T + 1],
                            scalar2=0.0, op0=ALU.mult, op1=ALU.add)
                    S.V('scalar_tensor_tensor', [ck, 'TAB', ('pI', r)], [ck], out=C[pr, h, :], in0=C[pr, h, :],
                        scalar=TAB[pr, 4 + d, h, T:T + 1], in1=pI[r][pr, 128:257], op0=ALU.mult, op1=ALU.add)

                def s3(i):
                    k, h, T, sl, r, readout, hp, pr = info(i)
                    if not readout:
                        return
                    S.T('matmul', [('ScT', r), ('Vw', r)], [('pI', r)], pI[r][:, 257:386], lhsT=ScT[:, r, :], rhs=Vw[:, r, :], start=True, stop=False)
                    S.T('matmul', [('kqT', sl), ('Cd16', r)], [('pI', r)], pI[r][:, 257:386], lhsT=kqT[pr, sl, 4 + hp, :], rhs=Cd16[pr, r, :],
                        start=False, stop=True)

                def s4(i):
                    k, h, T, sl, r, readout, hp, pr = info(i)
                    if not readout:
                        return
                    S.A('activation', [('pI', r)], [('dn', r)], out=dn[:, r, 0:1], in_=pI[r][:, 385:386], func=AF.Abs)

                def s5(i):
                    k, h, T, sl, r, readout, hp, pr = info(i)
                    if not readout:
                        return
                    S.V('tensor_tensor', [('dn', r), 'TAB'], [('dn', r)], out=dn[:, r, 1:2], in0=dn[:, r, 0:1], in1=TAB[:, 2 + d, h, T:T + 1], op=ALU.max)
                    S.V('reciprocal', [('dn', r)], [('dn', r)], out=dn[:, r, 2:3], in_=dn[:, r, 1:2])

                def s6(i):
                    k, h, T, sl, r, readout, hp, pr = info(i)
                    if not readout:
                        return
                    S.A('activation', [('pI', r), ('dn', r)], [('Hout', sl)], out=Hout[:, sl, h * 128:(h + 1) * 128], in_=pI[r][:, 257:385],
                        func=AF.Copy, scale=dn[:, r, 2:3])
                    if h == 7 and d == 1:
                        S.V('tensor_tensor', [('Hout', sl), ('Hin', sl)], [('Hout', sl)], out=Hout[:, sl, :], in0=Hout[:, sl, :], in1=Hin[:, sl, :], op=ALU.add)
                    if h == 7:
                        S.DMA('sync', [('Hout', sl)], [('HFo', d, T)], out=(HF if d == 0 else HS)[T - 2], in_=Hout[:, sl, :])

                loads3(0)
                transp3(0)
                NI3 = len(items)
                stages = (s0, s1, s2, s3, s4, s5, s6)
                for n in range(NI3 + len(stages) - 1):
                    for j, st in enumerate(stages):
                        if 0 <= n - j < NI3:
                            st(n - j)
                S.barrier()

    if debug and dump:
        if 2 in phases:
            S.DMA('sync', ['TAB'], ['GDBG'], out=GDBG[:, 0:48 * NT], in_=TAB[:].rearrange("p a h t -> p (a h t)"))
        else:
            S.DMA('sync', ['GT'], ['GDBG'], out=GDBG[:, 0:32 * NT], in_=GT[:].rearrange("p j t -> p (j t)"))
        S.DMA('sync', ['MV'], ['GDBG'], out=GDBG[:, 48 * NT:48 * NT + 64], in_=MV[:].rearrange("p a b -> p (a b)"))
        S.barrier()
    mid.close()

    if 4 in phases:
        S.barrier()
        with contextlib.ExitStack() as ps:
            sb4 = lambda name, shape, dt=F32: ps.enter_context(nc.sbuf_tensor(name, list(shape), dt))
            BTb = sb4("BTb", [128, 4, 16, 128], BF16)
            Kct = sb4("Kct", [128, 2, 1024], BF16)
            KcT = sb4("KcT", [128, 2, 8, 128], BF16)
            VcA = sb4("VcA", [128, 2, 16, 65], BF16)
            Qt = sb4("Qt", [128, 2, 1024], BF16)
            Kt = sb4("Kt", [128, 2, 4, 1024], BF16)
            Vt = sb4("Vt", [128, 2, 4, 16, 65], BF16)
            Mkb = sb4("Mkb", [128, 2, 4, 128], BF16)
            BMb = sb4("BMb", [128, 2, 16, 4, 128], BF16)
            QT = sb4("QT", [128, 2, 8, 128], BF16)
            KT = sb4("KT", [128, 2, 4, 8, 128], BF16)
            Pm = sb4("Pm", [128, 3, 4, 128], BF16)
            Pc = sb4("Pc", [128, 3, 2, 128], BF16)
            NAo = sb4("NAo", [128, 2, 1024], BF16)
            rc4 = sb4("rc4", [128, 3, 2])
            pT4 = ps.enter_context(nc.psum_tensor("pT4", [128, 2, 8, 128], BF16))
            pS4 = [ps.enter_context(nc.psum_tensor("pS4_%d" % i, [128, 4, 128], F32)) for i in range(2)]
            pC4 = [ps.enter_context(nc.psum_tensor("pC4_%d" % i, [128, 512], F32)) for i in range(2)]
            pO4 = [ps.enter_context(nc.psum_tensor("pO4_%d" % i, [128, 512], F32)) for i in range(2)]
            for a_ in range(4):
                S.DMA('gpsimd', [], [('BTb', a_)], out=BTb[:, a_].rearrange("p h q -> p (h q)"), in_=biasT[:, a_ * 2048:(a_ + 1) * 2048])
            BTkeys = [('BTb', a_) for a_ in range(4)]
            tq = [0]
            for j in range(2):
                S.DMA('sync', [], [('Kct', j)], out=Kct[:, j, :], in_=KCB[j * 128:(j + 1) * 128, :])
                S.DMA('sync', [], [('VcA', j)], out=VcA[:, j].rearrange("p h e -> p (h e)"), in_=VCB[j * 128:(j + 1) * 128, :])
            for j in range(2):
                for hp in range(8):
                    S.T('transpose', [('Kct', j), 'identb'], [('pT4', tq[0] % 2)], out=pT4[:, tq[0] % 2, hp, :], in_=Kct[:, j, hp * 128:(hp + 1) * 128],
                        identity=identb[:])
                S.A('activation', [('pT4', tq[0] % 2)], [('KcT', j)], out=KcT[:, j], in_=pT4[:, tq[0] % 2], func=AF.Copy)
                tq[0] += 1

            def loads4(blk):
                rb, cb = blk // 4, blk % 4
                sl = blk % 2
                S.DMA('sync', [], [('Qt', sl)], out=Qt[:, sl, :], in_=QB[8 * rb:8 * rb + 8, 16 * cb:16 * cb + 16, :])
                S.DMA('gpsimd', [], [('Mkb', sl)], out=Mkb[:, sl].rearrange("p a q -> p (a q)"), in_=maskT[blk])
                for j in range(4):
                    S.DMA('sync', [], [('Kt', sl, j)], out=Kt[:, sl, j, :], in_=KB[8 * rb + 4 * j:8 * rb + 4 * j + 4, 16 * cb:16 * cb + 32, :])
                    S.DMA('sync', [], [('Vt', sl, j)], out=Vt[:, sl, j].rearrange("p h e -> p (h e)"),
                          in_=VB[8 * rb + 4 * j:8 * rb + 4 * j + 4, 16 * cb:16 * cb + 32, :])

            def transp4(blk, part):
                sl = blk % 2
                t_ = tq[0] % 2
                tq[0] += 1
                if part == 0:
                    for hp in range(8):
                        S.T('transpose', [('Qt', sl), 'identb'], [('pT4', t_)], out=pT4[:, t_, hp, :], in_=Qt[:, sl, hp * 128:(hp + 1) * 128],
                            identity=identb[:])
                    S.V('tensor_copy', [('pT4', t_)], [('QT', sl)], out=QT[:, sl], in_=pT4[:, t_])
                else:
                    j = part - 1
                    for hp in range(8):
                        S.T('transpose', [('Kt', sl, j), 'identb'], [('pT4', t_)], out=pT4[:, t_, hp, :],
                            in_=Kt[:, sl, j, hp * 128:(hp + 1) * 128], identity=identb[:])
                    S.V('tensor_copy', [('pT4', t_)], [('KT', sl, j)], out=KT[:, sl, j], in_=pT4[:, t_])

            def combine4(blk, h):
                sl = blk % 2
                eng = 'vector' if h % 2 == 0 else 'gpsimd'
                S.op(eng, lambda e: e.tensor_tensor(out=BMb[:, sl, h], in0=BTb[:, :, h, :], in1=Mkb[:, sl], op=ALU.add),
                     BTkeys + [('Mkb', sl)], [('BMb', sl, h)])

            def inf4(i):
                blk, h = i // 16, i % 16
                hp, po = h // 2, 64 * (h % 2)
                return blk, h, blk % 2, hp, slice(po, po + 64)

            def q0(i):
                blk, h, sl, hp, pr = inf4(i)
                r2 = i % 2
                if blk + 1 < nblk and h in (5, 7, 9, 11, 13):
                    transp4(blk + 1, (h - 5) // 2)
                if blk + 1 < nblk and h >= 3:
                    combine4(blk + 1, h - 3)
                    if h == 15:
                        for h_ in (13, 14, 15):
                            combine4(blk + 1, h_)
                S.T('matmul', ['identb', ('BMb', sl, h)], [('pS4', r2)], pS4[r2][:].rearrange("p a q -> p (a q)"), lhsT=identb[:],
                    rhs=BMb[:, sl, h].rearrange("p a q -> p (a q)"), start=True, stop=False)
                for j in range(4):
                    S.T('matmul', [('KT', sl, j), ('QT', sl)], [('pS4', r2)], pS4[r2][:, j, :], lhsT=KT[pr, sl, j, hp, :], rhs=QT[pr, sl, hp, :],
                        start=False, stop=(j == 3))
                for j in range(2):
                    S.T('matmul', [('KcT', j), ('QT', sl)], [('pC4', r2)], pC4[r2][:, j * 128:(j + 1) * 128],
                        lhsT=KcT[pr, j, hp, :], rhs=QT[pr, sl, hp, :], start=True, stop=True)

            def q1(i):
                blk, h, sl, hp, pr = inf4(i)
                r2, r3 = i % 2, i % 3
                S.A('activation', [('pS4', r2)], [('Pm', r3)], out=Pm[:, r3], in_=pS4[r2][:], func=AF.Exp)
                S.A('activation', [('pC4', r2)], [('Pc', r3)], out=Pc[:, r3].rearrange("p a q -> p (a q)"), in_=pC4[r2][:, 0:256], func=AF.Exp)

            def q2(i):
                blk, h, sl, hp, pr = inf4(i)
                r2, r3 = i % 2, i % 3
                for j in range(4):
                    S.T('matmul', [('Pm', r3), ('Vt', sl, j)], [('pO4', r2)], pO4[r2][:, 0:65], lhsT=Pm[:, r3, j, :], rhs=Vt[:, sl, j, h, :],
                        start=(j == 0), stop=False)
                for j in range(2):
                    S.T('matmul', [('Pc', r3), ('VcA', j)], [('pO4', r2)], pO4[r2][:, 0:65], lhsT=Pc[:, r3, j, :], rhs=VcA[:, j, h, :],
                        start=False, stop=(j == 1))

            def q3(i):
                blk, h, sl, hp, pr = inf4(i)
                r2, r3 = i % 2, i % 3
                S.V('reciprocal', [('pO4', r2)], [('rc4', r3)], out=rc4[:, r3, 0:1], in_=pO4[r2][:, 64:65])
                S.V('tensor_scalar', [('pO4', r2), ('rc4', r3)], [('NAo', sl)], out=NAo[:, sl, h * 64:(h + 1) * 64], in0=pO4[r2][:, 0:64],
                    scalar1=rc4[:, r3, 0:1], scalar2=None, op0=ALU.mult)
                if h == 15:
                    rb, cb = blk // 4, blk % 4
                    if blk + 2 < nblk:
                        loads4(blk + 2)
                    S.DMA('sync', [('NAo', sl)], [('NA', blk)], out=NA[8 * rb:8 * rb + 8, 16 * cb:16 * cb + 16, :], in_=NAo[:, sl, :])

            NI = nblk * 16
            loads4(0)
            if nblk > 1:
                loads4(1)
            for part in range(5):
                transp4(0, part)
            for h_ in range(16):
                combine4(0, h_)
            stages4 = (q0, q1, q2, q3)
            for n in range(NI + len(stages4) - 1):
                for j in range(len(stages4) - 1, -1, -1):
                    if 0 <= n - j < NI:
                        stages4[j](n - j)
        S.barrier()

    NTOK = 256
    NTL = OWN // NTOK

    def fm_norm_a(srcT, skey, hT, hkey):
        for kc in range(8):
            S.A('activation', [skey], [hkey], out=hT[:, kc, :], in_=srcT[:, kc, :], func=AF.Square)

    def fm_norm_b(ps_bank, srcT, skey, hT, hkey, rstdB, tmpf, gmi, shi):
        for kc in range(8):
            S.T('matmul', [hkey, 'onesb'], ['pR'], ps_bank[:, 0:NTOK], lhsT=onesb[:], rhs=hT[:, kc, :], start=(kc == 0), stop=(kc == 7))
        S.A('activation', ['pR'], ['rstdB'], out=rstdB[:], in_=ps_bank[:, 0:NTOK], func=AF.Ln, scale=1.0 / D, bias=epsb[:, 0:1])
        S.A('activation', ['rstdB'], ['rstdB'], out=rstdB[:], in_=rstdB[:], func=AF.Exp, scale=-0.5)
        for kc in range(8):
            S.V('scalar_tensor_tensor', [skey, 'MV', 'rstdB'], [('tmpf', kc % 2)], out=tmpf[:, kc % 2, :], in0=srcT[:, kc, :], scalar=MV[:, gmi, kc:kc + 1],
                in1=rstdB[:], op0=ALU.mult, op1=ALU.mult)
            S.A('activation', [('tmpf', kc % 2), 'MV'], [hkey], out=hT[:, kc, :], in_=tmpf[:, kc % 2, :], func=AF.Identity, bias=MV[:, shi, kc:kc + 1])

    if 5 in phases:
        with contextlib.ExitStack() as ps:
            sb5 = lambda name, shape, dt=F32: ps.enter_context(nc.sbuf_tensor(name, list(shape), dt))
            W5 = sb5("W5", [128, 8, 3072], BF16)
            Wa = sb5("Wa", [128, 8, 1024], BF16)
            Wb = sb5("Wb", [128, 8, 1024], BF16)
            Wo = sb5("Wo", [128, 8, 1024], BF16)
            gA = sb5("gA", [128, 1024])
            xt5 = sb5("xt5", [128, 2, 1024])
            xT32 = sb5("xT32", [128, 2, 8, NTOK])
            hxT5 = sb5("hxT5", [128, 2, 8, NTOK], BF16)
            rstdB = sb5("rstdB", [128, NTOK])
            tmpf = sb5("tmpf", [128, 2, NTOK])
            sgT = sb5("sgT", [128, 16, NTOK], BF16)
            Ht = sb5("Ht", [128, 2, 1024])
            NAt = sb5("NAt", [128, 2, 1024], BF16)
            so = sb5("so", [128, 2, 1024])
            sqh = sb5("sqh", [128, 1024])
            t2 = sb5("t2", [128, 2, 1024])
            ha = sb5("ha", [128, 2, 1024], BF16)
            st5 = sb5("st5", [128, 3, 8])
            haT = sb5("haT", [128, 8, NTOK], BF16)
            naT = sb5("naT", [128, 8, NTOK], BF16)
            mixT = sb5("mixT", [128, 8, NTOK], BF16)
            tA = sb5("tA", [128, 2, NTOK])
            tB = sb5("tB", [128, 2, NTOK])
            xn = sb5("xn", [128, 2, NTOK])
            pX = [ps.enter_context(nc.psum_tensor("pX%d" % i, [128, 512], F32)) for i in range(2)]
            pR = ps.enter_context(nc.psum_tensor("pR5", [128, 512], F32))
            pM = [ps.enter_context(nc.psum_tensor("pM%d" % i, [128, 512], F32)) for i in range(4)]
            pT5 = ps.enter_context(nc.psum_tensor("pT5", [128, 8, 128], BF16))
            for kc in range(8):
                for hf in range(2):
                    S.DMA('gpsimd', [], [('W5m', kc, hf)], out=W5[:, kc, 1024 + hf * 1024:2048 + hf * 1024],
                          in_=w_in[kc * 128:(kc + 1) * 128, 6176 + hf * 1024:7200 + hf * 1024])
            for kc in range(8):
                S.DMA('gpsimd', [], [('W5o', kc)], out=W5[:, kc, 0:1024], in_=w_in[kc * 128:(kc + 1) * 128, 2048:3072])
            for kc in range(8):
                S.DMA('gpsimd', [], [('Wa', kc)], out=Wa[:, kc, :], in_=w_a[kc * 128:(kc + 1) * 128, :])
                S.DMA('gpsimd', [], [('Wb', kc)], out=Wb[:, kc, :], in_=w_b[kc * 128:(kc + 1) * 128, :])
            for kc in range(8):
                S.DMA('gpsimd', [], [('Wo', kc)], out=Wo[:, kc, :], in_=w_o[kc * 128:(kc + 1) * 128, :])
            S.DMA('sync', [], ['gA'], out=gA[:], in_=rowv[32:1056].partition_broadcast(128))
            NAf = NA.rearrange("r c f -> (r c) f")
            cnt = {'mi': 0, 'xi': 0}

            def bankM():
                b_ = cnt['mi'] % 4
                cnt['mi'] += 1
                return b_

            def loads5(i):
                t0 = i * NTOK
                for s_ in range(2):
                    tt = 2 * i + s_
                    S.DMA('sync', [], [('xt5', s_)], out=xt5[:, s_, :], in_=x[t0 + s_ * 128:t0 + (s_ + 1) * 128, :])
                    S.DMA('sync', [], [('Ht', s_)], out=Ht[:, s_, :], in_=HS[tt])
                    S.DMA('sync', [], [('NAt', s_)], out=NAt[:, s_, :], in_=NAf[tt * 128:(tt + 1) * 128, :])

            def stX(i):
                t0 = i * NTOK
                sl = i % 2
                for kc in range(8):
                    b_ = cnt['xi'] % 2
                    cnt['xi'] += 1
                    for s_ in range(2):
                        S.T('transpose', [('xt5', s_), 'cst'], [('pX', b_)], out=pX[b_][:, s_ * 128:(s_ + 1) * 128], in_=xt5[:, s_, kc * 128:(kc + 1) * 128],
                            identity=ident)
                    S.V('tensor_copy', [('pX', b_)], [('xT32', sl)], out=xT32[:, sl, kc, :], in_=pX[b_][:, 0:NTOK])
                fm_norm_a(xT32[:, sl], ('xT32', sl), hxT5[:, sl], ('hxT5', sl))

            def stXb(i):
                sl = i % 2
                fm_norm_b(pR, xT32[:, sl], ('xT32', sl), hxT5[:, sl], ('hxT5', sl), rstdB, tmpf, 0, 1)

            def stH(i, s_):
                S.A('activation', [('Ht', s_)], ['sqh'], out=sqh[:], in_=Ht[:, s_, :], func=AF.Square)
                S.V('tensor_reduce', ['sqh'], ['st5'], out=st5[:, 0, :], in_=sqh[:].rearrange("p (h e) -> p h e", h=8), axis=AX.X, op=ALU.add)
                S.A('activation', ['st5'], ['st5'], out=st5[:, 1, :], in_=st5[:, 0, :], func=AF.Ln, scale=1.0 / 128, bias=epsb[:, 0:1])
                S.A('activation', ['st5'], ['st5'], out=st5[:, 2, :], in_=st5[:, 1, :], func=AF.Exp, scale=-0.5)
                S.V('tensor_tensor', [('Ht', s_), 'st5'], ['sqh'], out=sqh[:].rearrange("p (h e) -> p h e", h=8), in0=Ht[:, s_, :].rearrange("p (h e) -> p h e", h=8),
                    in1=st5[:, 2, :].unsqueeze(2).to_broadcast([128, 8, 128]), op=ALU.mult)
                S.G('tensor_tensor', ['sqh', 'gA'], [('t2', s_)], out=t2[:, s_, :], in0=sqh[:], in1=gA[:], op=ALU.mult)
                for dc in range(8):
                    S.T('transpose', [('NAt', s_), 'identb'], ['pT5'], out=pT5[:, dc, :], in_=NAt[:, s_, dc * 128:(dc + 1) * 128], identity=identb[:])
                S.V('tensor_copy', ['pT5'], ['naT'], out=naT[:, :, s_ * 128:(s_ + 1) * 128], in_=pT5[:])

            def stM(i):
                sl = i % 2
                for n in range(16):
                    b_ = bankM()
                    for kc in range(8):
                        S.T('matmul', [('hxT5', sl), ('W5m', kc, n // 8)], [('pM', b_)], pM[b_][:, 0:NTOK], lhsT=W5[:, kc, 1024 + n * 128:1024 + (n + 1) * 128],
                            rhs=hxT5[:, sl, kc, :], start=(kc == 0), stop=(kc == 7))
                    S.A('activation', [('pM', b_)], ['sgT'], out=sgT[:, n, :], in_=pM[b_][:, 0:NTOK], func=AF.Sigmoid)

            def stOmm(i, s_):
                sl = i % 2
                for cg in range(2):
                    b_ = bankM()
                    for kc in range(8):
                        S.T('matmul', [('hxT5', sl), ('W5o', kc)], [('pM', b_)], pM[b_][:, 0:512], lhsT=hxT5[:, sl, kc, s_ * 128:(s_ + 1) * 128],
                            rhs=W5[:, kc, cg * 512:(cg + 1) * 512], start=(kc == 0), stop=(kc == 7))
                    S.A('activation', [('pM', b_)], [('so', s_)], out=so[:, s_, cg * 512:(cg + 1) * 512], in_=pM[b_][:, 0:512], func=AF.Sigmoid)
                S.V('tensor_tensor', [('t2', s_), ('so', s_)], [('ha', s_)], out=ha[:, s_, :], in0=t2[:, s_, :], in1=so[:, s_, :], op=ALU.mult)

            def stOtr(i, s_):
                for dc in range(8):
                    S.T('transpose', [('ha', s_), 'identb'], ['pT5'], out=pT5[:, dc, :], in_=ha[:, s_, dc * 128:(dc + 1) * 128], identity=identb[:])
                S.A('activation', ['pT5'], ['haT'], out=haT[:, :, s_ * 128:(s_ + 1) * 128], in_=pT5[:], func=AF.Copy)

            def stY(i):
                for n in range(8):
                    ba = bankM()
                    bb = bankM()
                    for dc in range(8):
                        S.T('matmul', ['haT', ('Wa', dc)], [('pM', ba)], pM[ba][:, 0:NTOK], lhsT=Wa[:, dc, n * 128:(n + 1) * 128], rhs=haT[:, dc, :],
                            start=(dc == 0), stop=(dc == 7))
                    for dc in range(8):
                        S.T('matmul', ['naT', ('Wb', dc)], [('pM', bb)], pM[bb][:, 0:NTOK], lhsT=Wb[:, dc, n * 128:(n + 1) * 128], rhs=naT[:, dc, :],
                            start=(dc == 0), stop=(dc == 7))
                    S.V('tensor_tensor', [('pM', ba), 'sgT'], [('tA', n % 2)], out=tA[:, n % 2, :], in0=pM[ba][:, 0:NTOK], in1=sgT[:, n, :], op=ALU.mult)
                    S.V('tensor_tensor', [('pM', bb), 'sgT'], [('tB', n % 2)], out=tB[:, n % 2, :], in0=pM[bb][:, 0:NTOK], in1=sgT[:, 8 + n, :], op=ALU.mult)
                    S.G('tensor_tensor', [('tA', n % 2), ('tB', n % 2)], ['mixT'], out=mixT[:, n, :], in0=tA[:, n % 2, :], in1=tB[:, n % 2, :], op=ALU.add)

            def stZ(i):
                t0 = i * NTOK
                sl = i % 2
                for n in range(8):
                    b_ = bankM()
                    for kc in range(8):
                        S.T('matmul', ['mixT', ('Wo', kc)], [('pM', b_)], pM[b_][:, 0:NTOK], lhsT=Wo[:, kc, n * 128:(n + 1) * 128], rhs=mixT[:, kc, :],
                            start=(kc == 0), stop=(kc == 7))
                    S.V('scalar_tensor_tensor', [('pM', b_), 'MV', ('xT32', sl)], [('xn', n % 2)], out=xn[:, n % 2, :], in0=pM[b_][:, 0:NTOK], scalar=MV[:, 4, n:n + 1],
                        in1=xT32[:, sl, n, :], op0=ALU.mult, op1=ALU.add)
                    S.DMA('sync', [('xn', n % 2)], [('XNT', i, n)], out=XNT[n, :, t0:t0 + NTOK], in_=xn[:, n % 2, :])

            loads5(0)
            stX(0)
            stXb(0)
            stH(0, 0)
            stH(0, 1)
            for i in range(NTL):
                if i + 1 < NTL:
                    loads5(i + 1)
                stOmm(i, 0)
                stOmm(i, 1)
                if i + 1 < NTL:
                    stX(i + 1)
                stM(i)
                stOtr(i, 0)
                stOtr(i, 1)
                if i + 1 < NTL:
                    stXb(i + 1)
                stY(i)
                if i + 1 < NTL:
                    stH(i + 1, 0)
                    stH(i + 1, 1)
                stZ(i)
        S.barrier()

    if 6 in phases:
        with contextlib.ExitStack() as ps:
            sb6 = lambda name, shape, dt=F32: ps.enter_context(nc.sbuf_tensor(name, list(shape), dt))
            Wg = sb6("Wg", [128, 8, DFF], BF16)
            Wu = sb6("Wu", [128, 8, DFF], BF16)
            Wd = sb6("Wd", [128, NFF, 1024], BF16)
            xnT = sb6("xnT", [128, 2, 8, NTOK])
            hx2T = sb6("hx2T", [128, 2, 8, NTOK], BF16)
            rstdB6 = sb6("rstdB6", [128, NTOK])
            tmpf6 = sb6("tmpf6", [128, 2, NTOK])
            sgl = sb6("sgl", [128, 2, NTOK])
            actT = sb6("actT", [128, NFF, NTOK], BF16)
            outT = sb6("outT", [128, 8, NTOK])
            ot = sb6("ot", [128, 2, 1024])
            pG = [ps.enter_context(nc.psum_tensor("pG%d" % i, [128, 512], F32)) for i in range(2)]
            pUu = [ps.enter_context(nc.psum_tensor("pUu%d" % i, [128, 512], F32)) for i in range(2)]
            pOo = [ps.enter_context(nc.psum_tensor("pOo%d" % i, [128, 512], F32)) for i in range(2)]
            pR6 = ps.enter_context(nc.psum_tensor("pR6", [128, 512], F32))
            pTo = ps.enter_context(nc.psum_tensor("pTo", [128, 512], F32))
            HFF = DFF // 2
            for hf in range(2):
                for kc in range(8):
                    S.DMA('gpsimd', [], [('Wg', kc, hf)], out=Wg[:, kc, hf * HFF:(hf + 1) * HFF], in_=w_g[kc * 128:(kc + 1) * 128, hf * HFF:(hf + 1) * HFF])
                    S.DMA('gpsimd', [], [('Wu', kc, hf)], out=Wu[:, kc, hf * HFF:(hf + 1) * HFF], in_=w_u[kc * 128:(kc + 1) * 128, hf * HFF:(hf + 1) * HFF])
            for m_ in range(NFF):
                S.DMA('gpsimd', [], [('Wd', m_)], out=Wd[:, m_, :], in_=w_d[m_ * 128:(m_ + 1) * 128, :])
            c6 = {'gi': 0, 'oi': 0}

            def loads6(i):
                t0 = i * NTOK
                sl = i % 2
                S.DMA('sync', [], [('xnT', sl)], out=xnT[:, sl], in_=XNT[:, :, t0:t0 + NTOK].rearrange("n p t -> p n t"))

            def stNa(i):
                sl = i % 2
                fm_norm_a(xnT[:, sl], ('xnT', sl), hx2T[:, sl], ('hx2T', sl))

            def stNb(i):
                sl = i % 2
                fm_norm_b(pR6, xnT[:, sl], ('xnT', sl), hx2T[:, sl], ('hx2T', sl), rstdB6, tmpf6, 5, 6)

            def stGU(i, m0, m1):
                sl = i % 2
                for m_ in range(m0, m1):
                    b_ = c6['gi'] % 2
                    c6['gi'] += 1
                    hf = m_ // 11
                    for kc in range(8):
                        S.T('matmul', [('hx2T', sl), ('Wg', kc, hf)], [('pG', b_)], pG[b_][:, 0:NTOK], lhsT=Wg[:, kc, m_ * 128:(m_ + 1) * 128], rhs=hx2T[:, sl, kc, :],
                            start=(kc == 0), stop=(kc == 7))
                    for kc in range(8):
                        S.T('matmul', [('hx2T', sl), ('Wu', kc, hf)], [('pUu', b_)], pUu[b_][:, 0:NTOK], lhsT=Wu[:, kc, m_ * 128:(m_ + 1) * 128], rhs=hx2T[:, sl, kc, :],
                            start=(kc == 0), stop=(kc == 7))
                    S.A('activation', [('pG', b_)], [('sgl', b_)], out=sgl[:, b_, :], in_=pG[b_][:, 0:NTOK], func=AF.Silu)
                    S.V('tensor_tensor', [('pUu', b_), ('sgl', b_)], ['actT'], out=actT[:, m_, :], in0=pUu[b_][:, 0:NTOK], in1=sgl[:, b_, :], op=ALU.mult)

            def stDN(i):
                sl = i % 2
                for n in range(8):
                    b_ = c6['oi'] % 2
                    c6['oi'] += 1
                    for m_ in range(NFF):
                        S.T('matmul', ['actT', ('Wd', m_)], [('pOo', b_)], pOo[b_][:, 0:NTOK], lhsT=Wd[:, m_, n * 128:(n + 1) * 128], rhs=actT[:, m_, :],
                            start=(m_ == 0), stop=(m_ == NFF - 1))
                    S.V('scalar_tensor_tensor', [('pOo', b_), 'MV', ('xnT', sl)], ['outT'], out=outT[:, n, :], in0=pOo[b_][:, 0:NTOK], scalar=MV[:, 7, n:n + 1],
                        in1=xnT[:, sl, n, :], op0=ALU.mult, op1=ALU.add)

            def stTR(i, s_, hfn):
                t0 = i * NTOK
                for n4 in range(4):
                    n = hfn * 4 + n4
                    S.T('transpose', ['outT', 'cst'], ['pTo'], out=pTo[:, n4 * 128:(n4 + 1) * 128], in_=outT[:, n, s_ * 128:(s_ + 1) * 128], identity=ident)
                S.A('activation', ['pTo'], [('ot', s_)], out=ot[:, s_, hfn * 512:(hfn + 1) * 512], in_=pTo[:], func=AF.Copy)
                if hfn == 1:
                    S.DMA('sync', [('ot', s_)], [('y', i, s_)], out=y[t0 + s_ * 128:t0 + (s_ + 1) * 128, :], in_=ot[:, s_, :])

            loads6(0)
            stNa(0)
            stNb(0)
            trp = ((0, 0), (0, 1), (1, 0), (1, 1))
            for i in range(NTL):
                if i + 1 < NTL:
                    loads6(i + 1)
                for m_ in range(NFF):
                    if i > 0 and m_ in (1, 3, 5, 7):
                        stTR(i - 1, *trp[(m_ - 1) // 2])
                    if i + 1 < NTL and m_ == 9:
                        stNa(i + 1)
                    if i + 1 < NTL and m_ == 14:
                        stNb(i + 1)
                    stGU(i, m_, m_ + 1)
                stDN(i)
            for tp_ in trp:
                stTR(NTL - 1, *tp_)
        S.barrier()

    S.emit()
    es.close()
    return nc


def _fm(v):
    return np.ascontiguousarray(v.reshape(-1, 128).T)


def _rope_table(flip):
    tab = np.zeros((NT * 128, 128), np.float32)
    tab[:, 0:32] = 1.0
    tab[:, 64:96] = 0.125
    t = np.arange(SEQ)
    tt = (SEQ - 1 - t) if flip else t
    row = (tt // 64).astype(np.float32)
    col = (tt % 64).astype(np.float32)
    inv = (10000.0 ** (-np.arange(16, dtype=np.float32) / 16)).astype(np.float32)
    ar = row[:, None] * inv[None, :]
    ac = col[:, None] * inv[None, :]
    cos = np.concatenate([np.cos(ar), np.cos(ac)], axis=1).astype(np.float32)
    sin = np.concatenate([np.sin(ar), np.sin(ac)], axis=1).astype(np.float32)
    tab[256:, 0:32] = cos
    tab[256:, 32:64] = sin
    tab[256:, 64:96] = cos * 0.125
    tab[256:, 96:128] = sin * 0.125
    return tab


def _consts():
    c = np.zeros((128, 4, 128), np.float32)
    c[:, 0, :] = np.eye(128)
    s = np.arange(128)
    c[:, 1, :] = (s[:, None] <= s[None, :])
    c[:, 2, :] = (s[:, None] >= s[None, :])
    c[:, 3, :] = 1.0
    return c.reshape(128, 512)


def _natten_tables(rpb, flip):
    a = np.arange(16)[:, None, None, None]
    e = np.arange(32)[None, :, None, None]
    i = np.arange(8)[None, None, :, None]
    j = np.arange(16)[None, None, None, :]
    drm = (a - i - 4) + 0 * e + 0 * j
    dcm = (e - j - 8) + 0 * a + 0 * i
    sgn = -1 if flip else 1
    dr = np.clip(sgn * drm + 7, 0, 14)
    dc = np.clip(sgn * dcm + 15, 0, 30)
    b = rpb[:, dr, dc]
    b = b.reshape(16, 4, 128, 128)
    bias = np.ascontiguousarray(b.transpose(2, 1, 0, 3)).reshape(128, 4 * 16 * 128).astype(np.float32)
    masks = np.zeros((32, 16, 32, 8, 16), np.float32)
    for rb in range(8):
        for cb in range(4):
            kr = 8 * rb - 4 + np.arange(16)
            kc = 16 * cb - 8 + np.arange(32)
            qr = 8 * rb + np.arange(8)
            qc = 16 * cb + np.arange(16)
            if flip:
                kr, kc, qr, qc = 127 - kr, 63 - kc, 127 - qr, 63 - qc
            wr = np.clip(qr - 4, 0, 120)
            wc = np.clip(qc - 8, 0, 48)
            inr = (kr[:, None] >= wr[None, :]) & (kr[:, None] < wr[None, :] + 8) & (kr[:, None] >= 0) & (kr[:, None] < 128)
            inc = (kc[:, None] >= wc[None, :]) & (kc[:, None] < wc[None, :] + 16) & (kc[:, None] >= 0) & (kc[:, None] < 64)
            masks[rb * 4 + cb] = (inr[:, None, :, None] & inc[None, :, None, :])
    masks = masks.reshape(32, 4, 128, 128).transpose(0, 2, 1, 3).reshape(32, 128, 512)
    return bias, np.ascontiguousarray((masks - 1.0) * 30000.0).astype(np.float32)


def prep_core(inp, core, shared):
    b, half = core // 2, core % 2
    flip = half == 1
    xs = inp['x'][b]
    cs = inp['ctx'][b]
    if flip:
        xs = xs[::-1]
        cs = cs[::-1]
    cvec = np.stack([_fm(inp['c'][b]), _fm(inp['c_ctx'])], axis=-1).astype(np.float32)
    m = {
        "x": np.ascontiguousarray(xs), "ctx": np.ascontiguousarray(cs), "cvec": np.ascontiguousarray(cvec),
        "w_mod": shared['w_mod'], "vecs": shared['vecs'],
        "w_in": shared['w_in_flip'] if flip else shared['w_in'],
        "rowv": shared['rowv_flip'] if flip else shared['rowv'],
        "rope": shared['rope_flip'] if flip else shared['rope'],
        "consts": shared['consts'],
        "biasT": shared['bias_flip'] if flip else shared['bias'],
        "maskT": shared['mask_flip'] if flip else shared['mask'],
        "w_a": shared['w_a'], "w_b": shared['w_b'], "w_o": shared['w_o'],
        "w_g": shared['w_g'], "w_u": shared['w_u'], "w_d": shared['w_d'],
    }
    return m


def prep_shared(inp):
    f = lambda a: np.ascontiguousarray(np.asarray(a, np.float32))
    w_in = f(inp['w_in'][0])
    w_in_flip = w_in.copy()
    w_in_flip[:, 3072:3088] = w_in[:, 3088:3104]
    w_in_flip[:, 3088:3104] = w_in[:, 3072:3088]
    bgt = f(inp['b_gates'][0]).reshape(32)
    bgt_flip = np.concatenate([bgt[16:32], bgt[0:16]])
    tail = np.concatenate([f(inp['mlstm_norm_g'][0]), f(inp['qn_g'][0]), f(inp['kn_g'][0])])
    rpb = f(inp['rpb'][0])
    bias, mask = _natten_tables(rpb, False)
    bias_f, mask_f = _natten_tables(rpb, True)
    sh = {
        'w_mod': f(inp['w_mod'][0]),
        'vecs': np.ascontiguousarray(np.concatenate([_fm(f(inp['b_mod'][0])), _fm(f(inp['norm1_g'][0])), _fm(f(inp['norm2_g'][0]))], axis=1)),
        'w_in': w_in, 'w_in_flip': w_in_flip,
        'rowv': np.concatenate([bgt, tail]), 'rowv_flip': np.concatenate([bgt_flip, tail]),
        'rope': _rope_table(False), 'rope_flip': _rope_table(True),
        'consts': _consts(),
        'bias': bias, 'mask': mask, 'bias_flip': bias_f, 'mask_flip': mask_f,
        'w_a': f(inp['w_branch_a'][0]), 'w_b': f(inp['w_branch_b'][0]), 'w_o': f(inp['w_out'][0]),
        'w_g': f(inp['w_ffn_gate'][0]), 'w_u': f(inp['w_ffn_up'][0]), 'w_d': f(inp['w_ffn_down'][0]),
    }
    return sh


def kernel(**inputs):
    inp = {k: np.asarray(v) for k, v in inputs.items()}
    shared = prep_shared(inp)
    in_maps = [prep_core(inp, c, shared) for c in range(8)]
    nc = build_nc()
    res = run_bass_kernel_spmd(nc, in_maps, core_ids=list(range(8)))
    out = np.zeros((4, SEQ, D), np.float32)
    for c in range(8):
        yc = np.asarray(res.results[c]["y"], np.float32)
        b, half = c // 2, c % 2
        if half == 0:
            out[b, 0:OWN] = yc
        else:
            out[b, OWN:] = yc[::-1]
    return out
```

```python
import contextlib
import numpy as np
import concourse.bass as bass
import concourse.mybir as mybir
from concourse.bass_utils import run_bass_kernel_spmd

F32 = mybir.dt.float32
BF16 = mybir.dt.bfloat16
AF = mybir.ActivationFunctionType
ALU = mybir.AluOpType
AX = mybir.AxisListType

COMPUTE = ('tensor', 'vector', 'scalar', 'gpsimd')
QUEUES = ('sync', 'gpsimd')

D = 1024
SEQ = 8192
OWN = 4096
NT = 66
EPS = 1e-6
DFF = 2816
NFF = 22


class Sched:
    def __init__(self, nc, n_dma_sems=24):
        self.nc = nc
        self.engs = ('tensor', 'vector', 'scalar', 'gpsimd', 'sync')
        self.ops = {e: [] for e in self.engs}
        self.count = {e: 0 for e in COMPUTE}
        self.sem = {e: nc.alloc_semaphore(name="sem_" + e) for e in COMPUTE}
        self.dsem = {q: [nc.alloc_semaphore(name="dsem_%s_%d" % (q, i)) for i in range(n_dma_sems)]
                     for q in QUEUES}
        self.dcount = {q: [0] * n_dma_sems for q in QUEUES}
        self.drr = {q: 0 for q in QUEUES}
        self.waited = {e: {} for e in self.engs}
        self.res = {}
        self.nops = 0

    def _semobj(self, key):
        if key[0] == 'eng':
            return self.sem[key[1]]
        return self.dsem[key[1]][key[2]]

    def _deps(self, eng, reads, writes):
        deps = {}

        def add(tok):
            if tok is None:
                return
            k, v = tok
            if deps.get(k, 0) < v:
                deps[k] = v
        for r in reads:
            st = self.res.get(r)
            if st is not None:
                add(st['w'])
        for w in writes:
            st = self.res.get(w)
            if st is not None:
                add(st['w'])
                for t in st['r']:
                    add(t)
        out = []
        wd = self.waited[eng]
        for k, v in deps.items():
            if k == ('eng', 'tensor') and eng == 'tensor':
                continue
            if wd.get(k, 0) >= v:
                continue
            wd[k] = v
            out.append((k, v))
        return out

    def _commit(self, tok, reads, writes):
        for r in reads:
            st = self.res.setdefault(r, {'w': None, 'r': []})
            st['r'].append(tok)
            if len(st['r']) > 32:
                best = {}
                for k, v in st['r']:
                    if best.get(k, 0) < v:
                        best[k] = v
                st['r'] = list(best.items())
        for w in writes:
            self.res[w] = {'w': tok, 'r': []}

    def op(self, eng, fn, reads=(), writes=()):
        waits = self._deps(eng, reads, writes)
        self.count[eng] += 1
        tok = (('eng', eng), self.count[eng])
        self.ops[eng].append((waits, fn, self.sem[eng], 1))
        self._commit(tok, reads, writes)
        self.nops += 1
        return tok

    def dma(self, q, fn, reads=(), writes=()):
        waits = self._deps(q, reads, writes)
        i = self.drr[q]
        self.drr[q] = (i + 1) % len(self.dsem[q])
        key = ('dma', q, i)
        prev = self.dcount[q][i]
        if prev > 0 and self.waited[q].get(key, 0) < prev:
            self.waited[q][key] = prev
            waits.append((key, prev))
        self.dcount[q][i] = prev + 16
        tok = (key, prev + 16)
        self.ops[q].append((waits, fn, self.dsem[q][i], 16))
        self._commit(tok, reads, writes)
        self.nops += 1
        return tok

    def T(self, name, reads, writes, *a, **kw):
        return self.op('tensor', lambda e: getattr(e, name)(*a, **kw), reads, writes)

    def V(self, name, reads, writes, *a, **kw):
        return self.op('vector', lambda e: getattr(e, name)(*a, **kw), reads, writes)

    def A(self, name, reads, writes, *a, **kw):
        return self.op('scalar', lambda e: getattr(e, name)(*a, **kw), reads, writes)

    def G(self, name, reads, writes, *a, **kw):
        return self.op('gpsimd', lambda e: getattr(e, name)(*a, **kw), reads, writes)

    def DMA(self, q, reads, writes, **kw):
        return self.dma(q, lambda e: e.dma_start(**kw), reads, writes)

    def barrier(self, skip_queues=()):
        toks = [(('eng', e), self.count[e]) for e in COMPUTE if self.count[e] > 0]
        for q in QUEUES:
            if q in skip_queues:
                continue
            for i, c in enumerate(self.dcount[q]):
                if c > 0:
                    toks.append((('dma', q, i), c))
        for e in self.engs:
            waits = []
            for k, v in toks:
                if k == ('eng', e):
                    continue
                if self.waited[e].get(k, 0) < v:
                    self.waited[e][k] = v
                    waits.append((k, v))
            if waits:
                self.ops[e].append((waits, None, None, 0))

    def check(self):
        val = {}
        pos = {e: 0 for e in self.engs}
        prog = True
        while prog:
            prog = False
            for e in self.engs:
                q = self.ops[e]
                while pos[e] < len(q):
                    waits, fn, sem, inc = q[pos[e]]
                    if any(val.get(k, 0) < v for k, v in waits):
                        break
                    if fn is not None:
                        k = None
                        for kk in ([('eng', x) for x in COMPUTE] + [('dma', qq, i) for qq in QUEUES for i in range(len(self.dsem[qq]))]):
                            if self._semobj(kk) is sem:
                                k = kk
                                break
                        val[k] = val.get(k, 0) + inc
                    pos[e] += 1
                    prog = True
        stuck = {e: (pos[e], len(self.ops[e])) for e in self.engs if pos[e] < len(self.ops[e])}
        if stuck:
            msg = []
            for e in stuck:
                waits = self.ops[e][pos[e]][0]
                msg.append("%s at %d/%d waits %s" % (e, pos[e], len(self.ops[e]), [(k, v, val.get(k, 0)) for k, v in waits if val.get(k, 0) < v]))
            raise RuntimeError("DEADLOCK: " + "; ".join(msg))

    def emit(self):
        nc = self.nc
        self.barrier()
        self.check()
        with nc.Block() as block:
            def mk(e):
                def body(eng):
                    for waits, fn, sem, inc in self.ops[e]:
                        for k, v in waits:
                            eng.wait_ge(self._semobj(k), v)
                        if fn is not None:
                            fn(eng).then_inc(sem, inc)
                return body
            block.tensor(mk('tensor'))
            block.vector(mk('vector'))
            block.scalar(mk('scalar'))
            block.gpsimd(mk('gpsimd'))
            block.sync(mk('sync'))


def build_nc(debug=False, phases=(0, 1, 2, 3, 4, 5, 6), nblk=32, dump=True, scr_in=()):
    nc = bass.Bass("TRN2", target_bir_lowering=False)
    S = Sched(nc)
    es = contextlib.ExitStack()

    def din(name, shape, dt=F32):
        return nc.dram_tensor(name, list(shape), dt, kind="ExternalInput").ap()

    def dscr(name, shape, dt):
        if name in scr_in:
            return nc.dram_tensor(name, list(shape), dt, kind="ExternalInput").ap()
        if debug:
            return nc.dram_tensor(name, list(shape), dt, kind="ExternalOutput").ap()
        return nc.dram_tensor(name, list(shape), dt).ap()

    x = din("x", [SEQ, D])
    ctx = din("ctx", [256, D])
    cvec = din("cvec", [128, 8, 2])
    w_mod = din("w_mod", [D, 6 * D])
    vecs = din("vecs", [128, 48 + 8 + 8])
    w_in = din("w_in", [D, 8224])
    rowv = din("rowv", [32 + 1024 + 64 + 64])
    rope = din("rope", [NT * 128, 128])
    consts = din("consts", [128, 4 * 128])
    biasT = din("biasT", [128, 4 * 16 * 128])
    maskT = din("maskT", [32, 128, 512])
    w_a = din("w_a", [D, D])
    w_b = din("w_b", [D, D])
    w_o = din("w_o", [D, D])
    w_g = din("w_g", [D, DFF])
    w_u = din("w_u", [D, DFF])
    w_d = din("w_d", [DFF, D])
    y = nc.dram_tensor("y", [OWN, D], F32, kind="ExternalOutput").ap()

    QA = dscr("QA", [32, 128, 512], BF16)
    KA = dscr("KA", [NT, 128, 512], BF16)
    VA = dscr("VA", [NT, 128, 8 * 129], BF16)
    QB = dscr("QB", [64, 64, 1024], BF16)
    KB = dscr("KB", [72, 80, 1024], BF16)
    VB = dscr("VB", [72, 80, 1040], BF16)
    KCB = dscr("KCB", [256, 1024], BF16)
    VCB = dscr("VCB", [256, 1040], BF16)
    HF = dscr("HF", [32, 128, 1024], F32)
    HS = dscr("HS", [32, 128, 1024], F32)
    NA = dscr("NA", [64, 64, 1024], BF16)
    XNT = dscr("XNT", [8, 128, OWN], F32)
    GDBG = dscr("GDBG", [128, 6 * 2 * 8 * NT], F32) if debug else None

    sb = lambda name, shape, dt=F32: es.enter_context(nc.sbuf_tensor(name, list(shape), dt))

    cst = sb("cst", [128, 4, 128])
    identb = sb("identb", [128, 128], BF16)
    onesb = sb("onesb", [128, 128], BF16)
    MV = sb("MV", [128, 8, 8])
    mhalf = sb("mhalf", [128, 1])
    epsb = sb("epsb", [128, 1])
    mid = contextlib.ExitStack()
    sbm = lambda name, shape, dt=F32: mid.enter_context(nc.sbuf_tensor(name, list(shape), dt))
    GT = sbm("GT", [128, 32, NT])
    TAB = sbm("TAB", [128, 6, 8, NT])
    WA = sbm("WA", [128, 8, 5152], BF16)
    wsrc = [(0, 0, 1024), (1024, 1024, 1024), (2048, 3072, 32), (2080, 3104, 1024), (3104, 4128, 1024), (4128, 5152, 1024)]
    for (dc, sc_, n) in wsrc:
        for kc in range(8):
            S.DMA('gpsimd', [], [('WA', dc, kc)], out=WA[:, kc, dc:dc + n], in_=w_in[kc * 128:(kc + 1) * 128, sc_:sc_ + n])
    WAkeys = [('WA', dc, kc) for (dc, _, _) in wsrc for kc in range(8)]
    zt = sbm("zt", [128, 8320], BF16)
    S.G('memset', [], ['zt'], zt[:], 0.0)
    S.DMA('sync', [], ['cst'], out=cst[:], in_=consts.rearrange("p (a b) -> p a b", a=4))
    S.V('tensor_copy', ['cst'], ['identb'], out=identb[:], in_=cst[:, 0, :])
    S.V('tensor_copy', ['cst'], ['onesb'], out=onesb[:], in_=cst[:, 3, :])
    S.G('memset', [], ['mhalf'], mhalf[:], -0.5)
    S.G('memset', [], ['epsb'], epsb[:], EPS)
    ident = cst[:, 0, :]


    cv = sbm("cv", [128, 8, 2])
    sv = sbm("sv", [128, 8, 2])
    vt = sbm("vt", [128, 64])
    modT = sbm("modT", [128, 48, 2])
    tmp8 = sbm("tmp8", [128, 8])
    n1g = vt[:, 48:56]
    n2g = vt[:, 56:64]

    def mod_blocks(blks, wm, pm, epi=True):
        for blk in blks:
            sl = blk % 2
            S.DMA('sync', [], [('wm', sl)], out=wm[:, sl, :, :],
                  in_=w_mod[:, blk * 512:(blk + 1) * 512].rearrange("(k p) n -> p k n", p=128))
            for jj in range(4):
                j = blk * 4 + jj
                for kc in range(8):
                    S.T('matmul', [('wm', sl), 'sv'], ['pm'], pm[:, j, :], lhsT=wm[:, sl, kc, jj * 128:(jj + 1) * 128],
                        rhs=sv[:, kc, :], start=(kc == 0), stop=(kc == 7))
        if not epi:
            return
        j0, j1 = blks[0] * 4, blks[-1] * 4 + 4
        S.V('tensor_tensor', ['pm', 'vt'], ['modT'], out=modT[:, j0:j1, :], in0=pm[:, j0:j1, :],
            in1=vt[:, j0:j1].unsqueeze(2).to_broadcast([128, j1 - j0, 2]), op=ALU.add)

    def mod_scale(dst, src, gg):
        S.V('tensor_scalar', ['modT'], ['tmp8'], out=tmp8[:], in0=src, scalar1=1.0, scalar2=None, op0=ALU.add)
        S.V('tensor_tensor', ['tmp8', 'vt'], ['MV'], out=MV[:, dst, :], in0=tmp8[:], in1=gg, op=ALU.mult)

    if 0 in phases:
        with contextlib.ExitStack() as ps:
            wm = ps.enter_context(nc.sbuf_tensor("wm", [128, 2, 8, 512], F32))
            pm = ps.enter_context(nc.psum_tensor("pm", [128, 48, 2], F32))
            S.DMA('sync', [], ['cv'], out=cv[:], in_=cvec)
            S.DMA('sync', [], ['vt'], out=vt[:], in_=vecs)
            S.A('activation', ['cv'], ['sv'], out=sv[:], in_=cv[:], func=AF.Silu)
            mod_blocks([0, 1, 2, 3], wm, pm)
            mod_scale(0, modT[:, 8:16, 0], n1g)
            mod_scale(2, modT[:, 8:16, 1], n1g)
            S.V('tensor_copy', ['modT'], ['MV'], out=MV[:, 1, :], in_=modT[:, 0:8, 0])
            S.V('tensor_copy', ['modT'], ['MV'], out=MV[:, 3, :], in_=modT[:, 0:8, 1])
        S.barrier(skip_queues=('gpsimd',))

    if 1 in phases:
        with contextlib.ExitStack() as ps:
            sb1 = lambda name, shape, dt=F32: ps.enter_context(nc.sbuf_tensor(name, list(shape), dt))
            xt = sb1("xt", [128, 4, 1024])
            junk = sb1("junk", [128, 1024], BF16)
            xs = sb1("xs", [128, 3, 1024], BF16)
            hxT = sb1("hxT", [128, 2, 8, 128], BF16)
            st1 = sb1("st1", [128, 3, 4])
            rp = sb1("rp", [128, 4, 128])
            bg = sb1("bg", [128, 32])
            gq = sb1("gq", [128, 64])
            gk = sb1("gk", [128, 64])
            QAo = sb1("QAo", [128, 2, 512], BF16)
            KAo = sb1("KAo", [128, 2, 512], BF16)
            VAo = sb1("VAo", [128, 2, 8, 129], BF16)
            QBo = sb1("QBo", [128, 2, 1024], BF16)
            KBo = sb1("KBo", [128, 2, 1024], BF16)
            VBo = sb1("VBo", [128, 2, 16, 65], BF16)
            rA = sb1("rA", [128, 4, 256])
            sq = sb1("sq", [128, 2, 512])
            tmpn = sb1("tmpn", [128, 2, 512])
            st8 = sb1("st8", [128, 2, 3, 8])
            pT = ps.enter_context(nc.psum_tensor("pT1", [128, 2, 8, 128], BF16))
            pP = [ps.enter_context(nc.psum_tensor("pP%d" % i, [128, 512], F32)) for i in range(6)]

            S.DMA('sync', [], ['bg'], out=bg[:], in_=rowv[0:32].partition_broadcast(128))
            S.DMA('sync', [], ['gq'], out=gq[:], in_=rowv[1056:1120].partition_broadcast(128))
            S.DMA('sync', [], ['gk'], out=gk[:], in_=rowv[1120:1184].partition_broadcast(128))
            S.V('tensor_scalar', ['gq'], ['gq'], out=gq[:], in0=gq[:], scalar1=0.125, scalar2=None, op0=ALU.mult)
            S.G('memset', [], [('VAo', 0), ('VAo', 1)], VAo[:, :, :, 128:129], 1.0)
            S.G('memset', [], [('VBo', 0), ('VBo', 1)], VBo[:, :, :, 64:65], 1.0)

            ppi = [0]

            def proj(sl, c0, n, reads_extra=()):
                i = ppi[0] % 6
                ppi[0] += 1
                dcs = [dc for (dc, _, nn) in wsrc if dc < c0 + n and c0 < dc + nn]
                for kc in range(8):
                    S.T('matmul', [('hxT', sl)] + [('WA', dc, kc) for dc in dcs], [('pP', i)], pP[i][:, 0:n], lhsT=hxT[:, sl, kc, :],
                        rhs=WA[:, kc, c0:c0 + n], start=(kc == 0), stop=(kc == 7))
                return pP[i], ('pP', i)

            def rope_epi(pt, pk, sl, out_tile, okey, tab_off, s3=0):
                v = pt[:, 0:512].rearrange("p (h a f i) -> p h a f i", h=8, a=2, f=2, i=16)
                x1 = v[:, :, :, 0, :]
                x2 = v[:, :, :, 1, :]
                cosb = rp[:, s3, tab_off:tab_off + 32].rearrange("p (a i) -> p a i", a=2).unsqueeze(1).to_broadcast([128, 8, 2, 16])
                sinb = rp[:, s3, tab_off + 32:tab_off + 64].rearrange("p (a i) -> p a i", a=2).unsqueeze(1).to_broadcast([128, 8, 2, 16])
                tv = [rA[:, i, :].rearrange("p (h a i) -> p h a i", h=8, a=2) for i in range(4)]
                S.V('tensor_tensor', [pk, ('rp', s3)], [('rA', 0)], out=tv[0], in0=x1, in1=cosb, op=ALU.mult)
                S.V('tensor_tensor', [pk, ('rp', s3)], [('rA', 1)], out=tv[1], in0=x2, in1=sinb, op=ALU.mult)
                S.V('tensor_tensor', [pk, ('rp', s3)], [('rA', 2)], out=tv[2], in0=x2, in1=cosb, op=ALU.mult)
                S.V('tensor_tensor', [pk, ('rp', s3)], [('rA', 3)], out=tv[3], in0=x1, in1=sinb, op=ALU.mult)
                ov = out_tile.rearrange("p (h a f i) -> p h a f i", h=8, a=2, f=2, i=16)
                S.G('tensor_tensor', [('rA', 0), ('rA', 1)], [okey], out=ov[:, :, :, 0, :], in0=tv[0], in1=tv[1], op=ALU.subtract)
                S.G('tensor_tensor', [('rA', 2), ('rA', 3)], [okey], out=ov[:, :, :, 1, :], in0=tv[2], in1=tv[3], op=ALU.add)

            def qknorm_epi(pt, pk, sl, g_tile, gkey, out_ap, okey):
                s2 = ppi[0] % 2
                S.A('activation', [pk], [('sq', s2)], out=sq[:, s2, :], in_=pt[:, 0:512], func=AF.Square)
                S.V('tensor_reduce', [('sq', s2)], [('st8', s2)], out=st8[:, s2, 0, :], in_=sq[:, s2, :].rearrange("p (h d) -> p h d", h=8),
                    axis=AX.X, op=ALU.add)
                S.V('tensor_scalar', [('st8', s2)], [('st8', s2)], out=st8[:, s2, 1, :], in0=st8[:, s2, 0, :], scalar1=1.0 / 64, scalar2=EPS,
                    op0=ALU.mult, op1=ALU.add)
                S.G('tensor_tensor', [('st8', s2), 'mhalf'], [('st8', s2)], out=st8[:, s2, 2, :], in0=st8[:, s2, 1, :],
                    in1=mhalf[:, 0:1].to_broadcast([128, 8]), op=ALU.pow)
                S.V('tensor_tensor', [pk, ('st8', s2)], [('tmpn', s2)], out=tmpn[:, s2, :].rearrange("p (h d) -> p h d", h=8),
                    in0=pt[:, 0:512].rearrange("p (h d) -> p h d", h=8),
                    in1=st8[:, s2, 2, :].unsqueeze(2).to_broadcast([128, 8, 64]), op=ALU.mult)
                S.G('tensor_tensor', [('tmpn', s2), gkey], [okey], out=out_ap.rearrange("p (h d) -> p h d", h=8),
                    in0=tmpn[:, s2, :].rearrange("p (h d) -> p h d", h=8),
                    in1=g_tile[:, 0:64].unsqueeze(1).to_broadcast([128, 8, 64]), op=ALU.mult)

            ORD = list(range(36, NT)) + [34, 35, 0, 1] + list(range(2, 34))
            POS = {T_: p_ for p_, T_ in enumerate(ORD)}

            def prepA0(T):
                s4 = POS[T] % 4
                src = ctx[T * 128:(T + 1) * 128, :] if T < 2 else x[(T - 2) * 128:(T - 1) * 128, :]
                S.DMA('sync', [], [('xt', s4)], out=xt[:, s4, :], in_=src)
                S.DMA('sync', [], [('rp', s4)], out=rp[:, s4, :], in_=rope[T * 128:(T + 1) * 128, :])

            def prepA(T):
                sl = POS[T] % 3
                s4 = POS[T] % 4
                S.A('activation', [('xt', s4)], ['junk', ('st1', sl)], out=junk[:], in_=xt[:, s4, :], func=AF.Square,
                    accum_out=st1[:, sl, 0:1])
                S.A('activation', [('st1', sl)], [('st1', sl)], out=st1[:, sl, 1:2], in_=st1[:, sl, 0:1], func=AF.Ln, scale=1.0 / D, bias=epsb[:, 0:1])
                S.A('activation', [('st1', sl)], [('st1', sl)], out=st1[:, sl, 2:3], in_=st1[:, sl, 1:2], func=AF.Exp, scale=-0.5)
                if 2 <= T < 34:
                    S.A('activation', [('xt', s4), ('st1', sl)], [('xs', sl)], out=xs[:, sl, :], in_=xt[:, s4, :], func=AF.Copy,
                        scale=st1[:, sl, 2:3])
                else:
                    S.V('tensor_scalar', [('xt', s4), ('st1', sl)], [('xs', sl)], out=xs[:, sl, :], in0=xt[:, s4, :], scalar1=st1[:, sl, 2:3],
                        scalar2=None, op0=ALU.mult)

            def prepB(T):
                sl = POS[T] % 2
                s3 = POS[T] % 3
                for kc in range(8):
                    S.T('transpose', [('xs', s3), 'identb'], [('pT', sl)], out=pT[:, sl, kc, :], in_=xs[:, s3, kc * 128:(kc + 1) * 128],
                        identity=identb[:])
                gi, si = (2, 3) if T < 2 else (0, 1)
                for kc in range(8):
                    if (2 <= T < 34) or kc % 2 == 0:
                        S.A('activation', [('pT', sl), 'MV'], [('hxT', sl)], out=hxT[:, sl, kc, :], in_=pT[:, sl, kc, :], func=AF.Identity,
                            scale=MV[:, gi, kc:kc + 1], bias=MV[:, si, kc:kc + 1])
                    else:
                        S.V('tensor_scalar', [('pT', sl), 'MV'], [('hxT', sl)], out=hxT[:, sl, kc, :], in0=pT[:, sl, kc, :],
                            scalar1=MV[:, gi, kc:kc + 1], scalar2=MV[:, si, kc:kc + 1], op0=ALU.mult, op1=ALU.add)

            def groups(T):
                sl = POS[T] % 2
                is_ctx = T < 2
                is_own = 2 <= T < 34
                halo = T in (34, 35)
                G = []

                def g_qa():
                    pt, pk = proj(sl, 0, 512)
                    rope_epi(pt, pk, sl, QAo[:, sl, :], ('QAo', sl), 64, POS[T] % 4)
                    S.DMA('sync', [('QAo', sl)], [('QA', T)], out=QA[T - 2], in_=QAo[:, sl, :])

                def g_ka():
                    pt, pk = proj(sl, 512, 512)
                    rope_epi(pt, pk, sl, KAo[:, sl, :], ('KAo', sl), 0, POS[T] % 4)
                    S.DMA('sync', [('KAo', sl)], [('KA', T)], out=KA[T], in_=KAo[:, sl, :])

                def g_va(hf):
                    def f():
                        pt, pk = proj(sl, 1024 + hf * 512, 512)
                        S.A('activation', [pk], [('VAo', sl)], out=VAo[:, sl, hf * 4:(hf + 1) * 4, 0:128],
                            in_=pt[:, 0:512].rearrange("p (h e) -> p h e", h=4), func=AF.Copy)
                        if hf == 1:
                            S.DMA('sync', [('VAo', sl)], [('VA', T)], out=VA[T], in_=VAo[:, sl, :, :].rearrange("p h e -> p (h e)"))
                    return f

                def g_gates():
                    pt, pk = proj(sl, 2048, 32)
                    S.V('tensor_tensor', [pk, 'bg'], ['GT'], out=GT[:, :, T], in0=pt[:, 0:32], in1=bg[:], op=ALU.add)

                def g_qb(hf):
                    def f():
                        pt, pk = proj(sl, 2080 + hf * 512, 512)
                        qknorm_epi(pt, pk, sl, gq, 'gq', QBo[:, sl, hf * 512:(hf + 1) * 512], ('QBo', sl))
                        if hf == 1:
                            r0 = (T - 2) * 2
                            for rr in range(2):
                                S.DMA('sync', [('QBo', sl)], [('QB', T)], out=QB[r0 + rr], in_=QBo[rr * 64:(rr + 1) * 64, sl, :])
                    return f

                def g_kb(hf):
                    def f():
                        pt, pk = proj(sl, 3104 + hf * 512, 512)
                        qknorm_epi(pt, pk, sl, gk, 'gk', KBo[:, sl, hf * 512:(hf + 1) * 512], ('KBo', sl))
                        if hf == 1:
                            if is_ctx:
                                S.DMA('sync', [('KBo', sl)], [('KCB', T)], out=KCB[T * 128:(T + 1) * 128, :], in_=KBo[:, sl, :])
                            else:
                                r0 = (T - 2) * 2 + 4
                                for rr in range(2):
                                    S.DMA('sync', [('KBo', sl)], [('KBs', T)], out=KB[r0 + rr, 8:72, :], in_=KBo[rr * 64:(rr + 1) * 64, sl, :])
                    return f

                def g_vb(hf):
                    def f():
                        pt, pk = proj(sl, 4128 + hf * 512, 512)
                        S.A('activation', [pk], [('VBo', sl)], out=VBo[:, sl, hf * 8:(hf + 1) * 8, 0:64],
                            in_=pt[:, 0:512].rearrange("p (h e) -> p h e", h=8), func=AF.Copy)
                        if hf == 1:
                            if is_ctx:
                                S.DMA('sync', [('VBo', sl)], [('VCB', T)], out=VCB[T * 128:(T + 1) * 128, :],
                                      in_=VBo[:, sl, :, :].rearrange("p h e -> p (h e)"))
                            else:
                                r0 = (T - 2) * 2 + 4
                                for rr in range(2):
                                    S.DMA('sync', [('VBo', sl)], [('VBs', T)], out=VB[r0 + rr, 8:72, :],
                                          in_=VBo[rr * 64:(rr + 1) * 64, sl, :, :].rearrange("p h e -> p (h e)"))
                    return f

                if is_own:
                    G.append(g_qa)
                G += [g_ka, g_va(0), g_va(1), g_gates]
                if is_own:
                    G += [g_qb(0), g_qb(1)]
                if is_own or halo or is_ctx:
                    G += [g_kb(0), g_kb(1), g_vb(0), g_vb(1)]
                return G

            for p_ in range(3):
                prepA0(ORD[p_])
            prepA(ORD[0])
            prepA(ORD[1])
            prepB(ORD[0])
            for p_ in range(NT):
                G = groups(ORD[p_])
                a_ = max(1, len(G) // 3)
                b_ = max(2, (2 * len(G)) // 3)
                for gi_, g in enumerate(G):
                    if gi_ == 0 and p_ + 3 < NT:
                        prepA0(ORD[p_ + 3])
                    if gi_ == a_ and p_ + 2 < NT:
                        prepA(ORD[p_ + 2])
                    if gi_ == b_ and p_ + 1 < NT:
                        prepB(ORD[p_ + 1])
                    g()
        S.barrier()


    Tf = [0, 1] + list(range(2, 34))
    Tb = [1, 0] + list(range(65, 33, -1)) + list(range(33, 1, -1))

    if 2 in phases:
        if 1 not in phases:
            S.V('memset', [], ['GT'], GT[:], 0.0)
        with contextlib.ExitStack() as ps:
            sb2 = lambda name, shape, dt=F32: ps.enter_context(nc.sbuf_tensor(name, list(shape), dt))
            TH = sb2("TH", [128, 32, NT])
            IG = sb2("IG", [128, 2, 8, NT])
            LF = sb2("LF", [128, 2, 8, NT])
            Bc = sb2("Bc", [128, 2, 8, NT])
            U = sb2("U", [128, 2, 8, NT])
            MB = sb2("MB", [128, 2, 8, NT])
            tmpg = sb2("tmpg", [128, 2, 8, NT])
            UmT = sb2("UmT", [128, 2, 8])
            TotT = sb2("TotT", [128, 2, 8])
            Um = sb2("Um", [8, 2, NT])
            Tot = sb2("Tot", [8, 2, NT])
            Mx = sb2("Mx", [8, 2, NT])
            MPT = sb2("MPT", [8, 2, NT])
            Dl = sb2("Dl", [8, 2, NT])
            Dg = sb2("Dg", [8, 2, 8, NT])
            pb = ps.enter_context(nc.psum_tensor("pb", [128, 4, 512], F32))
            ptr = [ps.enter_context(nc.psum_tensor("ptr%d" % i, [128, 512], F32)) for i in range(2)]
            psm = ps.enter_context(nc.psum_tensor("psm", [128, 512], F32))
            HN = 4 * NT
            if 0 in phases:
                wm2 = ps.enter_context(nc.sbuf_tensor("wm2", [128, 2, 8, 512], F32))
                pm2 = ps.enter_context(nc.psum_tensor("pm2", [128, 48, 2], F32))
            S.A('activation', ['GT'], ['TH'], out=TH[:], in_=GT[:], func=AF.Tanh, scale=1.0 / 15)
            for d in range(2):
                S.V('tensor_scalar', ['TH'], ['IG'], out=IG[:, d], in0=TH[:, 16 * d:16 * d + 8, :], scalar1=15.0, scalar2=None, op0=ALU.mult)
                S.A('activation', ['TH'], ['tmpg'], out=tmpg[:, d], in_=TH[:, 16 * d + 8:16 * d + 16, :], func=AF.Exp, scale=-15.0)
            S.A('activation', ['tmpg'], ['tmpg'], out=tmpg[:], in_=tmpg[:], func=AF.Ln, bias=1.0)
            S.V('tensor_scalar', ['tmpg'], ['LF'], out=LF[:], in0=tmpg[:], scalar1=-1.0, scalar2=None, op0=ALU.mult)
            for d in range(2):
                lf2 = LF[:, d].rearrange("p h t -> p (h t)")
                for hf in range(2):
                    S.T('matmul', ['LF', 'cst'], ['pb'], pb[:, 2 * d + hf, 0:HN], lhsT=cst[:, 1 + d, :], rhs=lf2[:, hf * HN:(hf + 1) * HN],
                        start=True, stop=True)
                    S.V('tensor_copy', ['pb'], ['Bc'], out=Bc[:, d].rearrange("p h t -> p (h t)")[:, hf * HN:(hf + 1) * HN],
                        in_=pb[:, 2 * d + hf, 0:HN])
            S.V('tensor_tensor', ['IG', 'Bc'], ['U'], out=U[:], in0=IG[:], in1=Bc[:], op=ALU.subtract)
            k = 0
            for d in range(2):
                for h in range(8):
                    ki = k % 2
                    pt_ = ptr[ki]
                    k += 1
                    S.T('transpose', ['U', 'cst'], [('ptr', ki)], out=pt_[0:NT, 0:128], in_=U[:, d, h, :], identity=ident)
                    S.V('tensor_reduce', [('ptr', ki)], ['UmT'], out=UmT[0:NT, d, h:h + 1], in_=pt_[0:NT, 0:128], axis=AX.X, op=ALU.max)
                    S.T('matmul', ['LF', 'cst'], ['psm'], psm[0:NT, 2 * (8 * d + h):2 * (8 * d + h) + 2], lhsT=LF[:, d, h, :], rhs=cst[:, 3, 0:2],
                        start=True, stop=True)
            S.V('tensor_copy', ['psm'], ['TotT'], out=TotT[0:NT].rearrange("p d h -> p (d h)"),
                in_=psm[0:NT, 0:32].rearrange("p (k two) -> p k two", two=2)[:, :, 0])
            for d in range(2):
                S.T('transpose', ['UmT', 'cst'], [('ptr', 0)], out=ptr[0][0:8, 0:NT], in_=UmT[0:NT, d, :], identity=ident[0:NT, 0:NT])
                S.V('tensor_copy', [('ptr', 0)], ['Um'], out=Um[:, d, :], in_=ptr[0][0:8, 0:NT])
                S.T('transpose', ['TotT', 'cst'], [('ptr', 1)], out=ptr[1][0:8, 0:NT], in_=TotT[0:NT, d, :], identity=ident[0:NT, 0:NT])
                S.V('tensor_copy', [('ptr', 1)], ['Tot'], out=Tot[:, d, :], in_=ptr[1][0:8, 0:NT])
            if 0 in phases:
                mod_blocks(list(range(4, 12)), wm2, pm2, epi=False)
            padkeys = []
            for (dst, w) in ((KB, 1024), (VB, 1040)):
                top = dst[0:4].rearrange("r c f -> (r c) f")
                for (a, b) in ((0, 128), (128, 256), (256, 320)):
                    padkeys.append(('pad', len(padkeys)))
                    S.DMA('sync', ['zt'], [padkeys[-1]], out=top[a:b, :], in_=zt[0:b - a, 0:w])
                padkeys.append(('pad', len(padkeys)))
                S.DMA('sync', ['zt'], [padkeys[-1]], out=dst[4:72, 0:8, :], in_=zt[0:68, 0:8 * w].rearrange("p (c f) -> p c f", c=8))
                padkeys.append(('pad', len(padkeys)))
                S.DMA('sync', ['zt'], [padkeys[-1]], out=dst[4:72, 72:80, :], in_=zt[0:68, 0:8 * w].rearrange("p (c f) -> p c f", c=8))

            S.V('memset', [], ['MPT'], MPT[:], 0.0)
            orders = (Tf, Tb)
            for step in range(NT):
                for d in range(2):
                    od = orders[d]
                    if step >= len(od):
                        continue
                    T = od[step]
                    S.V('tensor_tensor', ['MPT', 'Um'], ['Mx'], out=Mx[:, d, T:T + 1], in0=MPT[:, d, T:T + 1], in1=Um[:, d, T:T + 1], op=ALU.max)
                    if step + 1 < len(od):
                        Tn = od[step + 1]
                        S.V('tensor_tensor', ['Mx', 'Tot'], ['MPT'], out=MPT[:, d, Tn:Tn + 1], in0=Mx[:, d, T:T + 1], in1=Tot[:, d, T:T + 1], op=ALU.add)
            S.V('tensor_copy', ['Mx'], ['Mx'], out=Mx[:, 0, 34:NT], in_=Mx[:, 1, 34:NT])
            S.V('tensor_tensor', ['MPT', 'Mx'], ['Dl'], out=Dl[:], in0=MPT[:], in1=Mx[:], op=ALU.subtract)
            S.V('tensor_scalar', ['Dl'], ['Dl'], out=Dl[:], in0=Dl[:], scalar1=0.0, scalar2=None, op0=ALU.min)
            S.A('activation', ['Dl'], ['Dl'], out=Dl[:], in_=Dl[:], func=AF.Exp)
            eye8 = cst[0:8, 0, 0:8]
            for (srcv, skey, which) in ((Mx, 'Mx', 0), (Dl, 'Dl', 1)):
                for d in range(2):
                    S.V('tensor_tensor', [skey, 'cst'], ['Dg'], out=Dg[:, d], in0=srcv[:, d, :].unsqueeze(1).to_broadcast([8, 8, NT]),
                        in1=eye8.unsqueeze(2).to_broadcast([8, 8, NT]), op=ALU.mult)
                for d in range(2):
                    dg2 = Dg[:, d].rearrange("p h t -> p (h t)")
                    for hf in range(2):
                        S.T('matmul', ['Dg', 'cst'], ['pb'], pb[:, 2 * d + hf, 0:HN], lhsT=cst[0:8, 3, :], rhs=dg2[:, hf * HN:(hf + 1) * HN],
                            start=True, stop=True)
                        dstv = MB[:, d] if which == 0 else TAB[:, 4 + d]
                        S.V('tensor_copy', ['pb'], ['MB' if which == 0 else 'TAB'], out=dstv.rearrange("p h t -> p (h t)")[:, hf * HN:(hf + 1) * HN],
                            in_=pb[:, 2 * d + hf, 0:HN])
            S.V('tensor_tensor', ['U', 'MB'], ['tmpg'], out=tmpg[:], in0=U[:], in1=MB[:], op=ALU.subtract)
            S.A('activation', ['tmpg'], ['TAB'], out=TAB[:, 0:2], in_=tmpg[:], func=AF.Exp)
            S.V('tensor_tensor', ['Bc', 'MB'], ['tmpg'], out=tmpg[:], in0=Bc[:], in1=MB[:], op=ALU.add)
            S.A('activation', ['tmpg'], ['TAB'], out=TAB[:, 2:4], in_=tmpg[:], func=AF.Exp, scale=-1.0)
            if 0 in phases:
                S.V('tensor_tensor', ['pm', 'vt'], ['modT'], out=modT[:, 16:48, :], in0=pm2[:, 16:48, :],
                    in1=vt[:, 16:48].unsqueeze(2).to_broadcast([128, 32, 2]), op=ALU.add)
                mod_scale(5, modT[:, 32:40, 0], n2g)
                S.V('tensor_copy', ['modT'], ['MV'], out=MV[:, 4, :], in_=modT[:, 16:24, 0])
                S.V('tensor_copy', ['modT'], ['MV'], out=MV[:, 6, :], in_=modT[:, 24:32, 0])
                S.V('tensor_copy', ['modT'], ['MV'], out=MV[:, 7, :], in_=modT[:, 40:48, 0])
        S.barrier()

    if 3 in phases:
        with contextlib.ExitStack() as ps:
            sb3 = lambda name, shape, dt=F32: ps.enter_context(nc.sbuf_tensor(name, list(shape), dt))
            RD = 6
            KAt = sb3("KAt", [128, 3, 512], BF16)
            QAt = sb3("QAt", [128, 3, 512], BF16)
            VAt = sb3("VAt", [128, 3, 8, 129], BF16)
            kqT = sb3("kqT", [128, 3, 8, 128], BF16)
            ScT = sb3("ScT", [128, RD, 128], BF16)
            Vw = sb3("Vw", [128, RD, 129], BF16)
            C = sb3("C", [128, 8, 129])
            Cd16 = sb3("Cd16", [128, RD, 129], BF16)
            Hin = sb3("Hin", [128, 3, 1024])
            Hout = sb3("Hout", [128, 3, 1024])
            dn = sb3("dn", [128, RD, 4])
            pTk = ps.enter_context(nc.psum_tensor("pTk", [128, 2, 8, 128], BF16))
            pI = [ps.enter_context(nc.psum_tensor("pI%d" % i, [128, 512], F32)) for i in range(RD)]
            for d in range(2):
                order = (Tf, Tb)[d]
                S.V('memset', [], ['C%d' % h for h in range(8)], C[:], 0.0)
                items = [(k, h) for k in range(len(order)) for h in range(8)]

                def loads3(k):
                    T = order[k]
                    sl = k % 3
                    S.DMA('sync', [], [('KAt', sl)], out=KAt[:, sl, :], in_=KA[T])
                    S.DMA('sync', [], [('VAt', sl)], out=VAt[:, sl, :, :].rearrange("p h e -> p (h e)"), in_=VA[T])
                    if 2 <= T < 34:
                        S.DMA('sync', [], [('QAt', sl)], out=QAt[:, sl, :], in_=QA[T - 2])
                        if d == 1:
                            S.DMA('sync', [], [('Hin', sl)], out=Hin[:, sl, :], in_=HF[T - 2])

                def transp3(k):
                    T = order[k]
                    sl = k % 3
                    s2 = k % 2
                    if not (2 <= T < 34):
                        return
                    for hp in range(4):
                        S.T('transpose', [('KAt', sl), 'identb'], [('pTk', s2)], out=pTk[:, s2, hp, :],
                            in_=KAt[:, sl, hp * 128:(hp + 1) * 128], identity=identb[:])
                        S.T('transpose', [('QAt', sl), 'identb'], [('pTk', s2)], out=pTk[:, s2, 4 + hp, :],
                            in_=QAt[:, sl, hp * 128:(hp + 1) * 128], identity=identb[:])
                    S.A('activation', [('pTk', s2)], [('kqT', sl)], out=kqT[:, sl], in_=pTk[:, s2], func=AF.Copy)

                def info(i):
                    k, h = items[i]
                    T = order[k]
                    hp, po = h // 2, 64 * (h % 2)
                    return k, h, T, k % 3, i % RD, (2 <= T < 34), hp, slice(po, po + 64)

                def s0(i):
                    k, h, T, sl, r, readout, hp, pr = info(i)
                    if h == 0 and k + 1 < len(order):
                        loads3(k + 1)
                    if h == 4 and k + 1 < len(order):
                        transp3(k + 1)
                    S.G('tensor_scalar', [('VAt', sl), 'TAB'], [('Vw', r)], out=Vw[:, r, :], in0=VAt[:, sl, h, :], scalar1=TAB[:, d, h, T:T + 1],
                        scalar2=0.0, op0=ALU.mult, op1=ALU.add)

                def s1(i):
                    k, h, T, sl, r, readout, hp, pr = info(i)
                    S.T('matmul', [('KAt', sl), ('Vw', r)], [('pI', r)], pI[r][pr, 128:257], lhsT=KAt[:, sl, h * 64:(h + 1) * 64],
                        rhs=Vw[:, r, :], start=True, stop=True)
                    if readout:
                        S.T('matmul', [('kqT', sl)], [('pI', r)], pI[r][:, 0:128], lhsT=kqT[pr, sl, hp, :], rhs=kqT[pr, sl, 4 + hp, :],
                            start=True, stop=True)

                def s2(i):
                    k, h, T, sl, r, readout, hp, pr = info(i)
                    ck = 'C%d' % h
                    if readout:
                        S.V('tensor_tensor', [('pI', r), 'cst'], [('ScT', r)], out=ScT[:, r, :], in0=pI[r][:, 0:128], in1=cst[:, 1 + d, :], op=ALU.mult)
                        S.G('tensor_scalar', [ck, 'TAB'], [('Cd16', r)], out=Cd16[pr, r, :], in0=C[pr, h, :], scalar1=TAB[pr, 4 + d, h, T:T + 1],
                            scalar2=0.0, op0=ALU.mult, op1=ALU.add)
                    S.V('scalar_tensor_tensor', [ck, 'TAB', ('pI', r)], [ck], out=C[pr, h, :], in0=C[pr, h, :],
                        scalar=TAB[pr, 4 + d, h, T:T + 1], in1=pI[r][pr, 128:257], op0=ALU.mult, op1=ALU.add)

                def s3(i):
                    k, h, T, sl, r, readout, hp, pr = info(i)
                    if not readout:
                        return
                    S.T('matmul', [('ScT', r), ('Vw', r)], [('pI', r)], pI[r][:, 257:386], lhsT=ScT[:, r, :], rhs=Vw[:, r, :], start=True, stop=False)
                    S.T('matmul', [('kqT', sl), ('Cd16', r)], [('pI', r)], pI[r][:, 257:386], lhsT=kqT[pr, sl, 4 + hp, :], rhs=Cd16[pr, r, :],
                        start=False, stop=True)

                def s4(i):
                    k, h, T, sl, r, readout, hp, pr = info(i)
                    if not readout:
                        return
                    S.A('activation', [('pI', r)], [('dn', r)], out=dn[:, r, 0:1], in_=pI[r][:, 385:386], func=AF.Abs)

                def s5(i):
                    k, h, T, sl, r, readout, hp, pr = info(i)
                    if not readout:
                        return
                    S.V('tensor_tensor', [('dn', r), 'TAB'], [('dn', r)], out=dn[:, r, 1:2], in0=dn[:, r, 0:1], in1=TAB[:, 2 + d, h, T:T + 1], op=ALU.max)
                    S.V('reciprocal', [('dn', r)], [('dn', r)], out=dn[:, r, 2:3], in_=dn[:, r, 1:2])

                def s6(i):
                    k, h, T, sl, r, readout, hp, pr = info(i)
                    if not readout:
                        return
                    S.A('activation', [('pI', r), ('dn', r)], [('Hout', sl)], out=Hout[:, sl, h * 128:(h + 1) * 128], in_=pI[r][:, 257:385],
                        func=AF.Copy, scale=dn[:, r, 2:3])
                    if h == 7 and d == 1:
                        S.V('tensor_tensor', [('Hout', sl), ('Hin', sl)], [('Hout', sl)], out=Hout[:, sl, :], in0=Hout[:, sl, :], in1=Hin[:, sl, :], op=ALU.add)
                    if h == 7:
                        S.DMA('sync', [('Hout', sl)], [('HFo', d, T)], out=(HF if d == 0 else HS)[T - 2], in_=Hout[:, sl, :])

                loads3(0)
                transp3(0)
                NI3 = len(items)
                stages = (s0, s1, s2, s3, s4, s5, s6)
                for n in range(NI3 + len(stages) - 1):
                    for j, st in enumerate(stages):
                        if 0 <= n - j < NI3:
                            st(n - j)
                S.barrier()

    if debug and dump:
        if 2 in phases:
            S.DMA('sync', ['TAB'], ['GDBG'], out=GDBG[:, 0:48 * NT], in_=TAB[:].rearrange("p a h t -> p (a h t)"))
        else:
            S.DMA('sync', ['GT'], ['GDBG'], out=GDBG[:, 0:32 * NT], in_=GT[:].rearrange("p j t -> p (j t)"))
        S.DMA('sync', ['MV'], ['GDBG'], out=GDBG[:, 48 * NT:48 * NT + 64], in_=MV[:].rearrange("p a b -> p (a b)"))
        S.barrier()
    mid.close()

    if 4 in phases:
        S.barrier()
        with contextlib.ExitStack() as ps:
            sb4 = lambda name, shape, dt=F32: ps.enter_context(nc.sbuf_tensor(name, list(shape), dt))
            BTb = sb4("BTb", [128, 4, 16, 128], BF16)
            Kct = sb4("Kct", [128, 2, 1024], BF16)
            KcT = sb4("KcT", [128, 2, 8, 128], BF16)
            VcA = sb4("VcA", [128, 2, 16, 65], BF16)
            Qt = sb4("Qt", [128, 2, 1024], BF16)
            Kt = sb4("Kt", [128, 2, 4, 1024], BF16)
            Vt = sb4("Vt", [128, 2, 4, 16, 65], BF16)
            Mkb = sb4("Mkb", [128, 2, 4, 128], BF16)
            BMb = sb4("BMb", [128, 2, 16, 4, 128], BF16)
            QT = sb4("QT", [128, 2, 8, 128], BF16)
            KT = sb4("KT", [128, 2, 4, 8, 128], BF16)
            Pm = sb4("Pm", [128, 3, 4, 128], BF16)
            Pc = sb4("Pc", [128, 3, 2, 128], BF16)
            NAo = sb4("NAo", [128, 2, 1024], BF16)
            rc4 = sb4("rc4", [128, 3, 2])
            pT4 = ps.enter_context(nc.psum_tensor("pT4", [128, 2, 8, 128], BF16))
            pS4 = [ps.enter_context(nc.psum_tensor("pS4_%d" % i, [128, 4, 128], F32)) for i in range(2)]
            pC4 = [ps.enter_context(nc.psum_tensor("pC4_%d" % i, [128, 512], F32)) for i in range(2)]
            pO4 = [ps.enter_context(nc.psum_tensor("pO4_%d" % i, [128, 512], F32)) for i in range(2)]
            for a_ in range(4):
                S.DMA('gpsimd', [], [('BTb', a_)], out=BTb[:, a_].rearrange("p h q -> p (h q)"), in_=biasT[:, a_ * 2048:(a_ + 1) * 2048])
            BTkeys = [('BTb', a_) for a_ in range(4)]
            tq = [0]
            for j in range(2):
                S.DMA('sync', [], [('Kct', j)], out=Kct[:, j, :], in_=KCB[j * 128:(j + 1) * 128, :])
                S.DMA('sync', [], [('VcA', j)], out=VcA[:, j].rearrange("p h e -> p (h e)"), in_=VCB[j * 128:(j + 1) * 128, :])
            for j in range(2):
                for hp in range(8):
                    S.T('transpose', [('Kct', j), 'identb'], [('pT4', tq[0] % 2)], out=pT4[:, tq[0] % 2, hp, :], in_=Kct[:, j, hp * 128:(hp + 1) * 128],
                        identity=identb[:])
                S.A('activation', [('pT4', tq[0] % 2)], [('KcT', j)], out=KcT[:, j], in_=pT4[:, tq[0] % 2], func=AF.Copy)
                tq[0] += 1

            def loads4(blk):
                rb, cb = blk // 4, blk % 4
                sl = blk % 2
                S.DMA('sync', [], [('Qt', sl)], out=Qt[:, sl, :], in_=QB[8 * rb:8 * rb + 8, 16 * cb:16 * cb + 16, :])
                S.DMA('gpsimd', [], [('Mkb', sl)], out=Mkb[:, sl].rearrange("p a q -> p (a q)"), in_=maskT[blk])
                for j in range(4):
                    S.DMA('sync', [], [('Kt', sl, j)], out=Kt[:, sl, j, :], in_=KB[8 * rb + 4 * j:8 * rb + 4 * j + 4, 16 * cb:16 * cb + 32, :])
                    S.DMA('sync', [], [('Vt', sl, j)], out=Vt[:, sl, j].rearrange("p h e -> p (h e)"),
                          in_=VB[8 * rb + 4 * j:8 * rb + 4 * j + 4, 16 * cb:16 * cb + 32, :])

            def transp4(blk, part):
                sl = blk % 2
                t_ = tq[0] % 2
                tq[0] += 1
                if part == 0:
                    for hp in range(8):
                        S.T('transpose', [('Qt', sl), 'identb'], [('pT4', t_)], out=pT4[:, t_, hp, :], in_=Qt[:, sl, hp * 128:(hp + 1) * 128],
                            identity=identb[:])
                    S.V('tensor_copy', [('pT4', t_)], [('QT', sl)], out=QT[:, sl], in_=pT4[:, t_])
                else:
                    j = part - 1
                    for hp in range(8):
                        S.T('transpose', [('Kt', sl, j), 'identb'], [('pT4', t_)], out=pT4[:, t_, hp, :],
                            in_=Kt[:, sl, j, hp * 128:(hp + 1) * 128], identity=identb[:])
                    S.V('tensor_copy', [('pT4', t_)], [('KT', sl, j)], out=KT[:, sl, j], in_=pT4[:, t_])

            def combine4(blk, h):
                sl = blk % 2
                eng = 'vector' if h % 2 == 0 else 'gpsimd'
                S.op(eng, lambda e: e.tensor_tensor(out=BMb[:, sl, h], in0=BTb[:, :, h, :], in1=Mkb[:, sl], op=ALU.add),
                     BTkeys + [('Mkb', sl)], [('BMb', sl, h)])

            def inf4(i):
                blk, h = i // 16, i % 16
                hp, po = h // 2, 64 * (h % 2)
                return blk, h, blk % 2, hp, slice(po, po + 64)

            def q0(i):
                blk, h, sl, hp, pr = inf4(i)
                r2 = i % 2
                if blk + 1 < nblk and h in (5, 7, 9, 11, 13):
                    transp4(blk + 1, (h - 5) // 2)
                if blk + 1 < nblk and h >= 3:
                    combine4(blk + 1, h - 3)
                    if h == 15:
                        for h_ in (13, 14, 15):
                            combine4(blk + 1, h_)
                S.T('matmul', ['identb', ('BMb', sl, h)], [('pS4', r2)], pS4[r2][:].rearrange("p a q -> p (a q)"), lhsT=identb[:],
                    rhs=BMb[:, sl, h].rearrange("p a q -> p (a q)"), start=True, stop=False)
                for j in range(4):
                    S.T('matmul', [('KT', sl, j), ('QT', sl)], [('pS4', r2)], pS4[r2][:, j, :], lhsT=KT[pr, sl, j, hp, :], rhs=QT[pr, sl, hp, :],
                        start=False, stop=(j == 3))
                for j in range(2):
                    S.T('matmul', [('KcT', j), ('QT', sl)], [('pC4', r2)], pC4[r2][:, j * 128:(j + 1) * 128],
                        lhsT=KcT[pr, j, hp, :], rhs=QT[pr, sl, hp, :], start=True, stop=True)

            def q1(i):
                blk, h, sl, hp, pr = inf4(i)
                r2, r3 = i % 2, i % 3
                S.A('activation', [('pS4', r2)], [('Pm', r3)], out=Pm[:, r3], in_=pS4[r2][:], func=AF.Exp)
                S.A('activation', [('pC4', r2)], [('Pc', r3)], out=Pc[:, r3].rearrange("p a q -> p (a q)"), in_=pC4[r2][:, 0:256], func=AF.Exp)

            def q2(i):
                blk, h, sl, hp, pr = inf4(i)
                r2, r3 = i % 2, i % 3
                for j in range(4):
                    S.T('matmul', [('Pm', r3), ('Vt', sl, j)], [('pO4', r2)], pO4[r2][:, 0:65], lhsT=Pm[:, r3, j, :], rhs=Vt[:, sl, j, h, :],
                        start=(j == 0), stop=False)
                for j in range(2):
                    S.T('matmul', [('Pc', r3), ('VcA', j)], [('pO4', r2)], pO4[r2][:, 0:65], lhsT=Pc[:, r3, j, :], rhs=VcA[:, j, h, :],
                        start=False, stop=(j == 1))

            def q3(i):
                blk, h, sl, hp, pr = inf4(i)
                r2, r3 = i % 2, i % 3
                S.V('reciprocal', [('pO4', r2)], [('rc4', r3)], out=rc4[:, r3, 0:1], in_=pO4[r2][:, 64:65])
                S.V('tensor_scalar', [('pO4', r2), ('rc4', r3)], [('NAo', sl)], out=NAo[:, sl, h * 64:(h + 1) * 64], in0=pO4[r2][:, 0:64],
                    scalar1=rc4[:, r3, 0:1], scalar2=None, op0=ALU.mult)
                if h == 15:
                    rb, cb = blk // 4, blk % 4
                    if blk + 2 < nblk:
                        loads4(blk + 2)
                    S.DMA('sync', [('NAo', sl)], [('NA', blk)], out=NA[8 * rb:8 * rb + 8, 16 * cb:16 * cb + 16, :], in_=NAo[:, sl, :])

            NI = nblk * 16
            loads4(0)
            if nblk > 1:
                loads4(1)
            for part in range(5):
                transp4(0, part)
            for h_ in range(16):
                combine4(0, h_)
            stages4 = (q0, q1, q2, q3)
            for n in range(NI + len(stages4) - 1):
                for j, st in enumerate(stages4):
                    if 0 <= n - j < NI:
                        st(n - j)
        S.barrier()

    NTOK = 256
    NTL = OWN // NTOK

    def fm_norm_a(srcT, skey, hT, hkey):
        for kc in range(8):
            S.A('activation', [skey], [hkey], out=hT[:, kc, :], in_=srcT[:, kc, :], func=AF.Square)

    def fm_norm_b(ps_bank, srcT, skey, hT, hkey, rstdB, tmpf, gmi, shi):
        for kc in range(8):
            S.T('matmul', [hkey, 'onesb'], ['pR'], ps_bank[:, 0:NTOK], lhsT=onesb[:], rhs=hT[:, kc, :], start=(kc == 0), stop=(kc == 7))
        S.A('activation', ['pR'], ['rstdB'], out=rstdB[:], in_=ps_bank[:, 0:NTOK], func=AF.Ln, scale=1.0 / D, bias=epsb[:, 0:1])
        S.A('activation', ['rstdB'], ['rstdB'], out=rstdB[:], in_=rstdB[:], func=AF.Exp, scale=-0.5)
        for kc in range(8):
            S.V('scalar_tensor_tensor', [skey, 'MV', 'rstdB'], [('tmpf', kc % 2)], out=tmpf[:, kc % 2, :], in0=srcT[:, kc, :], scalar=MV[:, gmi, kc:kc + 1],
                in1=rstdB[:], op0=ALU.mult, op1=ALU.mult)
            S.A('activation', [('tmpf', kc % 2), 'MV'], [hkey], out=hT[:, kc, :], in_=tmpf[:, kc % 2, :], func=AF.Identity, bias=MV[:, shi, kc:kc + 1])

    if 5 in phases:
        with contextlib.ExitStack() as ps:
            sb5 = lambda name, shape, dt=F32: ps.enter_context(nc.sbuf_tensor(name, list(shape), dt))
            W5 = sb5("W5", [128, 8, 3072], BF16)
            Wa = sb5("Wa", [128, 8, 1024], BF16)
            Wb = sb5("Wb", [128, 8, 1024], BF16)
            Wo = sb5("Wo", [128, 8, 1024], BF16)
            gA = sb5("gA", [128, 1024])
            xt5 = sb5("xt5", [128, 2, 1024])
            xT32 = sb5("xT32", [128, 2, 8, NTOK])
            hxT5 = sb5("hxT5", [128, 2, 8, NTOK], BF16)
            rstdB = sb5("rstdB", [128, NTOK])
            tmpf = sb5("tmpf", [128, 2, NTOK])
            sgT = sb5("sgT", [128, 16, NTOK], BF16)
            Ht = sb5("Ht", [128, 2, 1024])
            NAt = sb5("NAt", [128, 2, 1024], BF16)
            so = sb5("so", [128, 2, 1024])
            sqh = sb5("sqh", [128, 1024])
            t2 = sb5("t2", [128, 2, 1024])
            ha = sb5("ha", [128, 2, 1024], BF16)
            st5 = sb5("st5", [128, 3, 8])
            haT = sb5("haT", [128, 8, NTOK], BF16)
            naT = sb5("naT", [128, 8, NTOK], BF16)
            mixT = sb5("mixT", [128, 8, NTOK], BF16)
            tA = sb5("tA", [128, 2, NTOK])
            tB = sb5("tB", [128, 2, NTOK])
            xn = sb5("xn", [128, 2, NTOK])
            pX = [ps.enter_context(nc.psum_tensor("pX%d" % i, [128, 512], F32)) for i in range(2)]
            pR = ps.enter_context(nc.psum_tensor("pR5", [128, 512], F32))
            pM = [ps.enter_context(nc.psum_tensor("pM%d" % i, [128, 512], F32)) for i in range(4)]
            pT5 = ps.enter_context(nc.psum_tensor("pT5", [128, 8, 128], BF16))
            for kc in range(8):
                for hf in range(2):
                    S.DMA('gpsimd', [], [('W5m', kc, hf)], out=W5[:, kc, 1024 + hf * 1024:2048 + hf * 1024],
                          in_=w_in[kc * 128:(kc + 1) * 128, 6176 + hf * 1024:7200 + hf * 1024])
            for kc in range(8):
                S.DMA('gpsimd', [], [('W5o', kc)], out=W5[:, kc, 0:1024], in_=w_in[kc * 128:(kc + 1) * 128, 2048:3072])
            for kc in range(8):
                S.DMA('gpsimd', [], [('Wa', kc)], out=Wa[:, kc, :], in_=w_a[kc * 128:(kc + 1) * 128, :])
                S.DMA('gpsimd', [], [('Wb', kc)], out=Wb[:, kc, :], in_=w_b[kc * 128:(kc + 1) * 128, :])
            for kc in range(8):
                S.DMA('gpsimd', [], [('Wo', kc)], out=Wo[:, kc, :], in_=w_o[kc * 128:(kc + 1) * 128, :])
            S.DMA('sync', [], ['gA'], out=gA[:], in_=rowv[32:1056].partition_broadcast(128))
            NAf = NA.rearrange("r c f -> (r c) f")
            cnt = {'mi': 0, 'xi': 0}

            def bankM():
                b_ = cnt['mi'] % 4
                cnt['mi'] += 1
                return b_

            def loads5(i):
                t0 = i * NTOK
                for s_ in range(2):
                    tt = 2 * i + s_
                    S.DMA('sync', [], [('xt5', s_)], out=xt5[:, s_, :], in_=x[t0 + s_ * 128:t0 + (s_ + 1) * 128, :])
                    S.DMA('sync', [], [('Ht', s_)], out=Ht[:, s_, :], in_=HS[tt])
                    S.DMA('sync', [], [('NAt', s_)], out=NAt[:, s_, :], in_=NAf[tt * 128:(tt + 1) * 128, :])

            def stX(i):
                t0 = i * NTOK
                sl = i % 2
                for kc in range(8):
                    b_ = cnt['xi'] % 2
                    cnt['xi'] += 1
                    for s_ in range(2):
                        S.T('transpose', [('xt5', s_), 'cst'], [('pX', b_)], out=pX[b_][:, s_ * 128:(s_ + 1) * 128], in_=xt5[:, s_, kc * 128:(kc + 1) * 128],
                            identity=ident)
                    S.V('tensor_copy', [('pX', b_)], [('xT32', sl)], out=xT32[:, sl, kc, :], in_=pX[b_][:, 0:NTOK])
                fm_norm_a(xT32[:, sl], ('xT32', sl), hxT5[:, sl], ('hxT5', sl))

            def stXb(i):
                sl = i % 2
                fm_norm_b(pR, xT32[:, sl], ('xT32', sl), hxT5[:, sl], ('hxT5', sl), rstdB, tmpf, 0, 1)

            def stH(i, s_):
                S.A('activation', [('Ht', s_)], ['sqh'], out=sqh[:], in_=Ht[:, s_, :], func=AF.Square)
                S.V('tensor_reduce', ['sqh'], ['st5'], out=st5[:, 0, :], in_=sqh[:].rearrange("p (h e) -> p h e", h=8), axis=AX.X, op=ALU.add)
                S.A('activation', ['st5'], ['st5'], out=st5[:, 1, :], in_=st5[:, 0, :], func=AF.Ln, scale=1.0 / 128, bias=epsb[:, 0:1])
                S.A('activation', ['st5'], ['st5'], out=st5[:, 2, :], in_=st5[:, 1, :], func=AF.Exp, scale=-0.5)
                S.V('tensor_tensor', [('Ht', s_), 'st5'], ['sqh'], out=sqh[:].rearrange("p (h e) -> p h e", h=8), in0=Ht[:, s_, :].rearrange("p (h e) -> p h e", h=8),
                    in1=st5[:, 2, :].unsqueeze(2).to_broadcast([128, 8, 128]), op=ALU.mult)
                S.G('tensor_tensor', ['sqh', 'gA'], [('t2', s_)], out=t2[:, s_, :], in0=sqh[:], in1=gA[:], op=ALU.mult)
                for dc in range(8):
                    S.T('transpose', [('NAt', s_), 'identb'], ['pT5'], out=pT5[:, dc, :], in_=NAt[:, s_, dc * 128:(dc + 1) * 128], identity=identb[:])
                S.V('tensor_copy', ['pT5'], ['naT'], out=naT[:, :, s_ * 128:(s_ + 1) * 128], in_=pT5[:])

            def stM(i):
                sl = i % 2
                for n in range(16):
                    b_ = bankM()
                    for kc in range(8):
                        S.T('matmul', [('hxT5', sl), ('W5m', kc, n // 8)], [('pM', b_)], pM[b_][:, 0:NTOK], lhsT=W5[:, kc, 1024 + n * 128:1024 + (n + 1) * 128],
                            rhs=hxT5[:, sl, kc, :], start=(kc == 0), stop=(kc == 7))
                    S.A('activation', [('pM', b_)], ['sgT'], out=sgT[:, n, :], in_=pM[b_][:, 0:NTOK], func=AF.Sigmoid)

            def stOmm(i, s_):
                sl = i % 2
                for cg in range(2):
                    b_ = bankM()
                    for kc in range(8):
                        S.T('matmul', [('hxT5', sl), ('W5o', kc)], [('pM', b_)], pM[b_][:, 0:512], lhsT=hxT5[:, sl, kc, s_ * 128:(s_ + 1) * 128],
                            rhs=W5[:, kc, cg * 512:(cg + 1) * 512], start=(kc == 0), stop=(kc == 7))
                    S.A('activation', [('pM', b_)], [('so', s_)], out=so[:, s_, cg * 512:(cg + 1) * 512], in_=pM[b_][:, 0:512], func=AF.Sigmoid)
                S.V('tensor_tensor', [('t2', s_), ('so', s_)], [('ha', s_)], out=ha[:, s_, :], in0=t2[:, s_, :], in1=so[:, s_, :], op=ALU.mult)

            def stOtr(i, s_):
                for dc in range(8):
                    S.T('transpose', [('ha', s_), 'identb'], ['pT5'], out=pT5[:, dc, :], in_=ha[:, s_, dc * 128:(dc + 1) * 128], identity=identb[:])
                S.A('activation', ['pT5'], ['haT'], out=haT[:, :, s_ * 128:(s_ + 1) * 128], in_=pT5[:], func=AF.Copy)

            def stY(i):
                for n in range(8):
                    ba = bankM()
                    bb = bankM()
                    for dc in range(8):
                        S.T('matmul', ['haT', ('Wa', dc)], [('pM', ba)], pM[ba][:, 0:NTOK], lhsT=Wa[:, dc, n * 128:(n + 1) * 128], rhs=haT[:, dc, :],
                            start=(dc == 0), stop=(dc == 7))
                    for dc in range(8):
                        S.T('matmul', ['naT', ('Wb', dc)], [('pM', bb)], pM[bb][:, 0:NTOK], lhsT=Wb[:, dc, n * 128:(n + 1) * 128], rhs=naT[:, dc, :],
                            start=(dc == 0), stop=(dc == 7))
                    S.V('tensor_tensor', [('pM', ba), 'sgT'], [('tA', n % 2)], out=tA[:, n % 2, :], in0=pM[ba][:, 0:NTOK], in1=sgT[:, n, :], op=ALU.mult)
                    S.V('tensor_tensor', [('pM', bb), 'sgT'], [('tB', n % 2)], out=tB[:, n % 2, :], in0=pM[bb][:, 0:NTOK], in1=sgT[:, 8 + n, :], op=ALU.mult)
                    S.G('tensor_tensor', [('tA', n % 2), ('tB', n % 2)], ['mixT'], out=mixT[:, n, :], in0=tA[:, n % 2, :], in1=tB[:, n % 2, :], op=ALU.add)

            def stZ(i):
                t0 = i * NTOK
                sl = i % 2
                for n in range(8):
                    b_ = bankM()
                    for kc in range(8):
                        S.T('matmul', ['mixT', ('Wo', kc)], [('pM', b_)], pM[b_][:, 0:NTOK], lhsT=Wo[:, kc, n * 128:(n + 1) * 128], rhs=mixT[:, kc, :],
                            start=(kc == 0), stop=(kc == 7))
                    S.V('scalar_tensor_tensor', [('pM', b_), 'MV', ('xT32', sl)], [('xn', n % 2)], out=xn[:, n % 2, :], in0=pM[b_][:, 0:NTOK], scalar=MV[:, 4, n:n + 1],
                        in1=xT32[:, sl, n, :], op0=ALU.mult, op1=ALU.add)
                    S.DMA('sync', [('xn', n % 2)], [('XNT', i, n)], out=XNT[n, :, t0:t0 + NTOK], in_=xn[:, n % 2, :])

            loads5(0)
            stX(0)
            stXb(0)
            stH(0, 0)
            stH(0, 1)
            for i in range(NTL):
                if i + 1 < NTL:
                    loads5(i + 1)
                stOmm(i, 0)
                stOmm(i, 1)
                if i + 1 < NTL:
                    stX(i + 1)
                stM(i)
                stOtr(i, 0)
                stOtr(i, 1)
                if i + 1 < NTL:
                    stXb(i + 1)
                stY(i)
                if i + 1 < NTL:
                    stH(i + 1, 0)
                    stH(i + 1, 1)
                stZ(i)
        S.barrier()

    if 6 in phases:
        with contextlib.ExitStack() as ps:
            sb6 = lambda name, shape, dt=F32: ps.enter_context(nc.sbuf_tensor(name, list(shape), dt))
            Wg = sb6("Wg", [128, 8, DFF], BF16)
            Wu = sb6("Wu", [128, 8, DFF], BF16)
            Wd = sb6("Wd", [128, NFF, 1024], BF16)
            xnT = sb6("xnT", [128, 2, 8, NTOK])
            hx2T = sb6("hx2T", [128, 2, 8, NTOK], BF16)
            rstdB6 = sb6("rstdB6", [128, NTOK])
            tmpf6 = sb6("tmpf6", [128, 2, NTOK])
            sgl = sb6("sgl", [128, 2, NTOK])
            actT = sb6("actT", [128, NFF, NTOK], BF16)
            outT = sb6("outT", [128, 8, NTOK])
            ot = sb6("ot", [128, 2, 1024])
            pG = [ps.enter_context(nc.psum_tensor("pG%d" % i, [128, 512], F32)) for i in range(2)]
            pUu = [ps.enter_context(nc.psum_tensor("pUu%d" % i, [128, 512], F32)) for i in range(2)]
            pOo = [ps.enter_context(nc.psum_tensor("pOo%d" % i, [128, 512], F32)) for i in range(2)]
            pR6 = ps.enter_context(nc.psum_tensor("pR6", [128, 512], F32))
            pTo = ps.enter_context(nc.psum_tensor("pTo", [128, 512], F32))
            HFF = DFF // 2
            for hf in range(2):
                for kc in range(8):
                    S.DMA('gpsimd', [], [('Wg', kc, hf)], out=Wg[:, kc, hf * HFF:(hf + 1) * HFF], in_=w_g[kc * 128:(kc + 1) * 128, hf * HFF:(hf + 1) * HFF])
                    S.DMA('gpsimd', [], [('Wu', kc, hf)], out=Wu[:, kc, hf * HFF:(hf + 1) * HFF], in_=w_u[kc * 128:(kc + 1) * 128, hf * HFF:(hf + 1) * HFF])
            for m_ in range(NFF):
                S.DMA('gpsimd', [], [('Wd', m_)], out=Wd[:, m_, :], in_=w_d[m_ * 128:(m_ + 1) * 128, :])
            c6 = {'gi': 0, 'oi': 0}

            def loads6(i):
                t0 = i * NTOK
                sl = i % 2
                S.DMA('sync', [], [('xnT', sl)], out=xnT[:, sl], in_=XNT[:, :, t0:t0 + NTOK].rearrange("n p t -> p n t"))

            def stNa(i):
                sl = i % 2
                fm_norm_a(xnT[:, sl], ('xnT', sl), hx2T[:, sl], ('hx2T', sl))

            def stNb(i):
                sl = i % 2
                fm_norm_b(pR6, xnT[:, sl], ('xnT', sl), hx2T[:, sl], ('hx2T', sl), rstdB6, tmpf6, 5, 6)

            def stGU(i, m0, m1):
                sl = i % 2
                for m_ in range(m0, m1):
                    b_ = c6['gi'] % 2
                    c6['gi'] += 1
                    hf = m_ // 11
                    for kc in range(8):
                        S.T('matmul', [('hx2T', sl), ('Wg', kc, hf)], [('pG', b_)], pG[b_][:, 0:NTOK], lhsT=Wg[:, kc, m_ * 128:(m_ + 1) * 128], rhs=hx2T[:, sl, kc, :],
                            start=(kc == 0), stop=(kc == 7))
                    for kc in range(8):
                        S.T('matmul', [('hx2T', sl), ('Wu', kc, hf)], [('pUu', b_)], pUu[b_][:, 0:NTOK], lhsT=Wu[:, kc, m_ * 128:(m_ + 1) * 128], rhs=hx2T[:, sl, kc, :],
                            start=(kc == 0), stop=(kc == 7))
                    S.A('activation', [('pG', b_)], [('sgl', b_)], out=sgl[:, b_, :], in_=pG[b_][:, 0:NTOK], func=AF.Silu)
                    S.V('tensor_tensor', [('pUu', b_), ('sgl', b_)], ['actT'], out=actT[:, m_, :], in0=pUu[b_][:, 0:NTOK], in1=sgl[:, b_, :], op=ALU.mult)

            def stDN(i):
                sl = i % 2
                for n in range(8):
                    b_ = c6['oi'] % 2
                    c6['oi'] += 1
                    for m_ in range(NFF):
                        S.T('matmul', ['actT', ('Wd', m_)], [('pOo', b_)], pOo[b_][:, 0:NTOK], lhsT=Wd[:, m_, n * 128:(n + 1) * 128], rhs=actT[:, m_, :],
                            start=(m_ == 0), stop=(m_ == NFF - 1))
                    S.V('scalar_tensor_tensor', [('pOo', b_), 'MV', ('xnT', sl)], ['outT'], out=outT[:, n, :], in0=pOo[b_][:, 0:NTOK], scalar=MV[:, 7, n:n + 1],
                        in1=xnT[:, sl, n, :], op0=ALU.mult, op1=ALU.add)

            def stTR(i, s_, hfn):
                t0 = i * NTOK
                for n4 in range(4):
                    n = hfn * 4 + n4
                    S.T('transpose', ['outT', 'cst'], ['pTo'], out=pTo[:, n4 * 128:(n4 + 1) * 128], in_=outT[:, n, s_ * 128:(s_ + 1) * 128], identity=ident)
                S.A('activation', ['pTo'], [('ot', s_)], out=ot[:, s_, hfn * 512:(hfn + 1) * 512], in_=pTo[:], func=AF.Copy)
                if hfn == 1:
                    S.DMA('sync', [('ot', s_)], [('y', i, s_)], out=y[t0 + s_ * 128:t0 + (s_ + 1) * 128, :], in_=ot[:, s_, :])

            loads6(0)
            stNa(0)
            stNb(0)
            trp = ((0, 0), (0, 1), (1, 0), (1, 1))
            for i in range(NTL):
                if i + 1 < NTL:
                    loads6(i + 1)
                for m_ in range(NFF):
                    if i > 0 and m_ in (1, 3, 5, 7):
                        stTR(i - 1, *trp[(m_ - 1) // 2])
                    if i + 1 < NTL and m_ == 9:
                        stNa(i + 1)
                    if i + 1 < NTL and m_ == 14:
                        stNb(i + 1)
                    stGU(i, m_, m_ + 1)
                stDN(i)
            for tp_ in trp:
                stTR(NTL - 1, *tp_)
        S.barrier()

    S.emit()
    es.close()
    return nc


def _fm(v):
    return np.ascontiguousarray(v.reshape(-1, 128).T)


def _rope_table(flip):
    tab = np.zeros((NT * 128, 128), np.float32)
    tab[:, 0:32] = 1.0
    tab[:, 64:96] = 0.125
    t = np.arange(SEQ)
    tt = (SEQ - 1 - t) if flip else t
    row = (tt // 64).astype(np.float32)
    col = (tt % 64).astype(np.float32)
    inv = (10000.0 ** (-np.arange(16, dtype=np.float32) / 16)).astype(np.float32)
    ar = row[:, None] * inv[None, :]
    ac = col[:, None] * inv[None, :]
    cos = np.concatenate([np.cos(ar), np.cos(ac)], axis=1).astype(np.float32)
    sin = np.concatenate([np.sin(ar), np.sin(ac)], axis=1).astype(np.float32)
    tab[256:, 0:32] = cos
    tab[256:, 32:64] = sin
    tab[256:, 64:96] = cos * 0.125
    tab[256:, 96:128] = sin * 0.125
    return tab


def _consts():
    c = np.zeros((128, 4, 128), np.float32)
    c[:, 0, :] = np.eye(128)
    s = np.arange(128)
    c[:, 1, :] = (s[:, None] <= s[None, :])
    c[:, 2, :] = (s[:, None] >= s[None, :])
    c[:, 3, :] = 1.0
    return c.reshape(128, 512)


def _natten_tables(rpb, flip):
    a = np.arange(16)[:, None, None, None]
    e = np.arange(32)[None, :, None, None]
    i = np.arange(8)[None, None, :, None]
    j = np.arange(16)[None, None, None, :]
    drm = (a - i - 4) + 0 * e + 0 * j
    dcm = (e - j - 8) + 0 * a + 0 * i
    sgn = -1 if flip else 1
    dr = np.clip(sgn * drm + 7, 0, 14)
    dc = np.clip(sgn * dcm + 15, 0, 30)
    b = rpb[:, dr, dc]
    b = b.reshape(16, 4, 128, 128)
    bias = np.ascontiguousarray(b.transpose(2, 1, 0, 3)).reshape(128, 4 * 16 * 128).astype(np.float32)
    masks = np.zeros((32, 16, 32, 8, 16), np.float32)
    for rb in range(8):
        for cb in range(4):
            kr = 8 * rb - 4 + np.arange(16)
            kc = 16 * cb - 8 + np.arange(32)
            qr = 8 * rb + np.arange(8)
            qc = 16 * cb + np.arange(16)
            if flip:
                kr, kc, qr, qc = 127 - kr, 63 - kc, 127 - qr, 63 - qc
            wr = np.clip(qr - 4, 0, 120)
            wc = np.clip(qc - 8, 0, 48)
            inr = (kr[:, None] >= wr[None, :]) & (kr[:, None] < wr[None, :] + 8) & (kr[:, None] >= 0) & (kr[:, None] < 128)
            inc = (kc[:, None] >= wc[None, :]) & (kc[:, None] < wc[None, :] + 16) & (kc[:, None] >= 0) & (kc[:, None] < 64)
            masks[rb * 4 + cb] = (inr[:, None, :, None] & inc[None, :, None, :])
    masks = masks.reshape(32, 4, 128, 128).transpose(0, 2, 1, 3).reshape(32, 128, 512)
    return bias, np.ascontiguousarray((masks - 1.0) * 30000.0).astype(np.float32)


def prep_core(inp, core, shared):
    b, half = core // 2, core % 2
    flip = half == 1
    xs = inp['x'][b]
    cs = inp['ctx'][b]
    if flip:
        xs = xs[::-1]
        cs = cs[::-1]
    cvec = np.stack([_fm(inp['c'][b]), _fm(inp['c_ctx'])], axis=-1).astype(np.float32)
    m = {
        "x": np.ascontiguousarray(xs), "ctx": np.ascontiguousarray(cs), "cvec": np.ascontiguousarray(cvec),
        "w_mod": shared['w_mod'], "vecs": shared['vecs'],
        "w_in": shared['w_in_flip'] if flip else shared['w_in'],
        "rowv": shared['rowv_flip'] if flip else shared['rowv'],
        "rope": shared['rope_flip'] if flip else shared['rope'],
        "consts": shared['consts'],
        "biasT": shared['bias_flip'] if flip else shared['bias'],
        "maskT": shared['mask_flip'] if flip else shared['mask'],
        "w_a": shared['w_a'], "w_b": shared['w_b'], "w_o": shared['w_o'],
        "w_g": shared['w_g'], "w_u": shared['w_u'], "w_d": shared['w_d'],
    }
    return m


def prep_shared(inp):
    f = lambda a: np.ascontiguousarray(np.asarray(a, np.float32))
    w_in = f(inp['w_in'][0])
    w_in_flip = w_in.copy()
    w_in_flip[:, 3072:3088] = w_in[:, 3088:3104]
    w_in_flip[:, 3088:3104] = w_in[:, 3072:3088]
    bgt = f(inp['b_gates'][0]).reshape(32)
    bgt_flip = np.concatenate([bgt[16:32], bgt[0:16]])
    tail = np.concatenate([f(inp['mlstm_norm_g'][0]), f(inp['qn_g'][0]), f(inp['kn_g'][0])])
    rpb = f(inp['rpb'][0])
    bias, mask = _natten_tables(rpb, False)
    bias_f, mask_f = _natten_tables(rpb, True)
    sh = {
        'w_mod': f(inp['w_mod'][0]),
        'vecs': np.ascontiguousarray(np.concatenate([_fm(f(inp['b_mod'][0])), _fm(f(inp['norm1_g'][0])), _fm(f(inp['norm2_g'][0]))], axis=1)),
        'w_in': w_in, 'w_in_flip': w_in_flip,
        'rowv': np.concatenate([bgt, tail]), 'rowv_flip': np.concatenate([bgt_flip, tail]),
        'rope': _rope_table(False), 'rope_flip': _rope_table(True),
        'consts': _consts(),
        'bias': bias, 'mask': mask, 'bias_flip': bias_f, 'mask_flip': mask_f,
        'w_a': f(inp['w_branch_a'][0]), 'w_b': f(inp['w_branch_b'][0]), 'w_o': f(inp['w_out'][0]),
        'w_g': f(inp['w_ffn_gate'][0]), 'w_u': f(inp['w_ffn_up'][0]), 'w_d': f(inp['w_ffn_down'][0]),
    }
    return sh


def kernel(**inputs):
    inp = {k: np.asarray(v) for k, v in inputs.items()}
    shared = prep_shared(inp)
    in_maps = [prep_core(inp, c, shared) for c in range(8)]
    nc = build_nc()
    res = run_bass_kernel_spmd(nc, in_maps, core_ids=list(range(8)))
    out = np.zeros((4, SEQ, D), np.float32)
    for c in range(8):
        yc = np.asarray(res.results[c]["y"], np.float32)
        b, half = c // 2, c % 2
        if half == 0:
            out[b, 0:OWN] = yc
        else:
            out[b, OWN:] = yc[::-1]
    return out
```
